# Optimizing a Trainium2 kernel written in Bass

```python
import math
import jax, jax.numpy as jnp
from jax import lax
import numpy as np

D_MODEL = 1024
BATCH = 8
SEQ = 4096
DEPTH = 4

HEAD_DIM = 64
ATTN_Q_HEADS = D_MODEL // (2 * HEAD_DIM)
ATTN_KV_HEADS = max(1, ATTN_Q_HEADS // 4)
RET_HEADS = D_MODEL // (2 * HEAD_DIM)
ATTN_WIDTH = ATTN_Q_HEADS * HEAD_DIM
KV_WIDTH = ATTN_KV_HEADS * HEAD_DIM
RET_WIDTH = RET_HEADS * HEAD_DIM
MIX_WIDTH = ATTN_WIDTH + RET_WIDTH
IN_SPLITS = (ATTN_WIDTH,
             ATTN_WIDTH + KV_WIDTH,
             ATTN_WIDTH + 2 * KV_WIDTH,
             ATTN_WIDTH + 2 * KV_WIDTH + RET_WIDTH,
             ATTN_WIDTH + 2 * KV_WIDTH + 2 * RET_WIDTH,
             ATTN_WIDTH + 2 * KV_WIDTH + 3 * RET_WIDTH)
IN_WIDTH = ATTN_WIDTH + 2 * KV_WIDTH + 4 * RET_WIDTH
WINDOW = 128
ATTN_BLOCK = 128
RET_CHUNK = 128
ROPE_THETA = 10000.0
RET_THETA = 10000.0
D_FF = ((8 * D_MODEL // 3 + 127) // 128) * 128
LN_EPS = 1e-5
GN_EPS = 1e-6
DEEPNORM_ALPHA = (2 * DEPTH) ** 0.25
DEEPNORM_BETA = (8 * DEPTH) ** -0.25

kernel_name = "hybrid_swa_sink_retention_macaron_deepnorm"


def layer_norm(x, g, b):
    xf = x.astype(jnp.float32)
    mu = jnp.mean(xf, axis=-1, keepdims=True)
    var = jnp.mean(jnp.square(xf - mu), axis=-1, keepdims=True)
    return ((xf - mu) * lax.rsqrt(var + LN_EPS)).astype(x.dtype) * g + b


def swiglu_ffn(x, w_gu, w_down):
    a, u = jnp.split(x @ w_gu, 2, axis=-1)
    return (jax.nn.silu(a) * u) @ w_down


def rope_tables(seq):
    pos = jnp.arange(seq, dtype=jnp.float32)
    inv_freq = ROPE_THETA ** (-jnp.arange(0, HEAD_DIM, 2, dtype=jnp.float32) / HEAD_DIM)
    ang = pos[:, None] * inv_freq[None, :]
    return jnp.cos(ang), jnp.sin(ang)


def retention_rotation_tables(seq):
    pos = jnp.arange(seq, dtype=jnp.float32)
    ang_freq = 1.0 / (RET_THETA ** jnp.linspace(0.0, 1.0, HEAD_DIM // 2, dtype=jnp.float32))
    ang = pos[:, None] * ang_freq[None, :]
    return jnp.cos(ang), jnp.sin(ang)


def apply_rope(x, cos, sin):
    c = cos[:, None, :].astype(x.dtype)
    s = sin[:, None, :].astype(x.dtype)
    x1, x2 = jnp.split(x, 2, axis=-1)
    return jnp.concatenate([x1 * c - x2 * s, x2 * c + x1 * s], axis=-1)


def apply_pair_rotation(x, cos, sin):
    c = cos[:, None, :].astype(x.dtype)
    s = sin[:, None, :].astype(x.dtype)
    xe = x[..., 0::2]
    xo = x[..., 1::2]
    return jnp.stack([xe * c - xo * s, xo * c + xe * s], axis=-1).reshape(x.shape)


def sliding_window_sink_attention(q, k, v, sinks):
    B, S, Hq, D = q.shape
    Hkv = k.shape[2]
    G = Hq // Hkv
    C = ATTN_BLOCK
    N = S // C
    qb = (q * (HEAD_DIM ** -0.5)).reshape(B, N, C, Hkv, G, D)

    def with_prev_block(t):
        tb = t.reshape(B, N, C, Hkv, D)
        prev = jnp.pad(tb, ((0, 0), (1, 0), (0, 0), (0, 0), (0, 0)))[:, :-1]
        return jnp.concatenate([prev, tb], axis=2)

    kk = with_prev_block(k)
    vv = with_prev_block(v)
    s = jnp.einsum('bnqhgd,bnkhd->bnhgqk', qb, kk).astype(jnp.float32)
    q_idx = jnp.arange(C)[:, None]
    k_rel = jnp.arange(2 * C)[None, :] - C
    rel = q_idx - k_rel
    band = (rel >= 0) & (rel < WINDOW)
    valid = (jnp.arange(N)[:, None] * C + k_rel) >= 0
    mask = band[None, :, :] & valid[:, None, :]
    s = jnp.where(mask[None, :, None, None, :, :], s, -jnp.inf)
    sink = sinks.astype(jnp.float32).reshape(Hkv, G)[None, None, :, :, None, None]
    m = jnp.maximum(jnp.max(s, axis=-1, keepdims=True), sink)
    p = jnp.exp(s - m)
    denom = jnp.sum(p, axis=-1, keepdims=True) + jnp.exp(sink - m)
    o = jnp.einsum('bnhgqk,bnkhd->bnqhgd', (p / denom).astype(v.dtype), vv)
    return o.reshape(B, S, Hq * D)


def multiscale_retention(q, k, v, gate):
    B, S, H, D = q.shape
    C = RET_CHUNK
    N = S // C
    log_gamma = jnp.log1p(-jnp.exp2(-5.0 - jnp.arange(H, dtype=jnp.float32)))
    idx = jnp.arange(C, dtype=jnp.float32)
    diff = idx[:, None] - idx[None, :]
    decay = jnp.where(diff[None] >= 0,
                      jnp.exp(jnp.maximum(diff, 0.0)[None] * log_gamma[:, None, None]), 0.0)
    w_k = jnp.exp((C - 1.0 - idx)[None, :] * log_gamma[:, None])
    w_q = jnp.exp((idx + 1.0)[None, :] * log_gamma[:, None])
    g_chunk = jnp.exp(C * log_gamma)
    qc = q.reshape(B, N, C, H, D)
    kc = k.reshape(B, N, C, H, D)
    vc = v.reshape(B, N, C, H, D)
    scores = jnp.einsum('bnihd,bnjhd->bnhij', qc, kc) * decay
    intra = jnp.einsum('bnhij,bnjhe->bnihe', scores, vc)
    kv = jnp.einsum('bnjhd,hj,bnjhe->bnhde', kc, w_k, vc)

    def step(state, kv_n):
        return state * g_chunk[None, :, None, None] + kv_n, state

    init = jnp.zeros((B, H, D, D), dtype=kv.dtype)
    _, states = lax.scan(step, init, jnp.moveaxis(kv, 1, 0))
    states = jnp.moveaxis(states, 0, 1)
    cross = jnp.einsum('bnihd,hi,bnhde->bnihe', qc, w_q, states)
    o = (intra + cross).reshape(B, S, H, D).astype(jnp.float32)
    mu = jnp.mean(o, axis=-1, keepdims=True)
    var = jnp.mean(jnp.square(o - mu), axis=-1, keepdims=True)
    o = ((o - mu) * lax.rsqrt(var + GN_EPS)).reshape(B, S, H * D).astype(gate.dtype)
    return jax.nn.silu(gate) * o


def hybrid_mixer(x, w_in, w_out, sinks, rope_cos, rope_sin, ret_cos, ret_sin):
    B, S, _ = x.shape
    h = x @ w_in
    qa, ka, va, qr, kr, vr, gr = jnp.split(h, IN_SPLITS, axis=-1)
    qa = apply_rope(qa.reshape(B, S, ATTN_Q_HEADS, HEAD_DIM), rope_cos, rope_sin)
    ka = apply_rope(ka.reshape(B, S, ATTN_KV_HEADS, HEAD_DIM), rope_cos, rope_sin)
    va = va.reshape(B, S, ATTN_KV_HEADS, HEAD_DIM)
    y_attn = sliding_window_sink_attention(qa, ka, va, sinks)
    qr = apply_pair_rotation(qr.reshape(B, S, RET_HEADS, HEAD_DIM), ret_cos, ret_sin)
    kr = apply_pair_rotation(kr.reshape(B, S, RET_HEADS, HEAD_DIM), ret_cos, ret_sin) * (HEAD_DIM ** -0.5)
    y_ret = multiscale_retention(qr, kr, vr.reshape(B, S, RET_HEADS, HEAD_DIM), gr)
    return jnp.concatenate([y_attn, y_ret], axis=-1) @ w_out


def setup_inputs(seed: int = 0) -> dict:
    key = jax.random.key(seed)
    ks = jax.random.split(key, 16)
    f32 = jnp.float32
    x = jax.random.normal(ks[0], (BATCH, SEQ, D_MODEL), f32)
    col_scale = jnp.concatenate([
        jnp.ones((ATTN_WIDTH + KV_WIDTH,), f32),
        jnp.full((KV_WIDTH,), DEEPNORM_BETA, f32),
        jnp.ones((2 * RET_WIDTH,), f32),
        jnp.full((RET_WIDTH,), DEEPNORM_BETA, f32),
        jnp.ones((RET_WIDTH,), f32)])
    w_in = jax.random.normal(ks[1], (DEPTH, D_MODEL, IN_WIDTH), f32) * (D_MODEL ** -0.5) * col_scale
    w_out = jax.random.normal(ks[2], (DEPTH, MIX_WIDTH, D_MODEL), f32) * (MIX_WIDTH ** -0.5) * DEEPNORM_BETA
    attn_sinks = 0.5 * jax.random.normal(ks[3], (DEPTH, ATTN_Q_HEADS), f32)
    ffn1_w_gu = jax.random.normal(ks[4], (DEPTH, D_MODEL, 2 * D_FF), f32) * (D_MODEL ** -0.5) * DEEPNORM_BETA
    ffn1_w_down = jax.random.normal(ks[5], (DEPTH, D_FF, D_MODEL), f32) * (D_FF ** -0.5) * DEEPNORM_BETA
    ffn2_w_gu = jax.random.normal(ks[6], (DEPTH, D_MODEL, 2 * D_FF), f32) * (D_MODEL ** -0.5) * DEEPNORM_BETA
    ffn2_w_down = jax.random.normal(ks[7], (DEPTH, D_FF, D_MODEL), f32) * (D_FF ** -0.5) * DEEPNORM_BETA
    ln1_g = 1.0 + 0.02 * jax.random.normal(ks[8], (DEPTH, D_MODEL), f32)
    ln1_b = 0.02 * jax.random.normal(ks[9], (DEPTH, D_MODEL), f32)
    ln2_g = 1.0 + 0.02 * jax.random.normal(ks[10], (DEPTH, D_MODEL), f32)
    ln2_b = 0.02 * jax.random.normal(ks[11], (DEPTH, D_MODEL), f32)
    ln3_g = 1.0 + 0.02 * jax.random.normal(ks[12], (DEPTH, D_MODEL), f32)
    ln3_b = 0.02 * jax.random.normal(ks[13], (DEPTH, D_MODEL), f32)
    return {"x": x, "w_in": w_in, "w_out": w_out, "attn_sinks": attn_sinks,
            "ffn1_w_gu": ffn1_w_gu, "ffn1_w_down": ffn1_w_down,
            "ffn2_w_gu": ffn2_w_gu, "ffn2_w_down": ffn2_w_down,
            "ln1_g": ln1_g, "ln1_b": ln1_b, "ln2_g": ln2_g, "ln2_b": ln2_b,
            "ln3_g": ln3_g, "ln3_b": ln3_b}


def reference(x, w_in, w_out, attn_sinks, ffn1_w_gu, ffn1_w_down, ffn2_w_gu, ffn2_w_down,
              ln1_g, ln1_b, ln2_g, ln2_b, ln3_g, ln3_b):
    S = x.shape[1]
    rope_cos, rope_sin = rope_tables(S)
    ret_cos, ret_sin = retention_rotation_tables(S)
    for l in range(DEPTH):
        x = layer_norm(DEEPNORM_ALPHA * x + 0.5 * swiglu_ffn(x, ffn1_w_gu[l], ffn1_w_down[l]),
                       ln1_g[l], ln1_b[l])
        x = layer_norm(DEEPNORM_ALPHA * x + hybrid_mixer(x, w_in[l], w_out[l], attn_sinks[l],
                                                         rope_cos, rope_sin, ret_cos, ret_sin),
                       ln2_g[l], ln2_b[l])
        x = layer_norm(DEEPNORM_ALPHA * x + 0.5 * swiglu_ffn(x, ffn2_w_gu[l], ffn2_w_down[l]),
                       ln3_g[l], ln3_b[l])
    return x
```

```python
import math
import os
from contextlib import ExitStack

import numpy as np
import concourse.bass as bass
import concourse.mybir as mybir
from concourse.bass_utils import run_bass_kernel_spmd

F32 = mybir.dt.float32
BF16 = mybir.dt.bfloat16
AF = mybir.ActivationFunctionType
ALU = mybir.AluOpType
AX = mybir.AxisListType

D = 1024
DEPTH = 4
SEQ = 4096
NB = 8
HD = 64
DFF = 2816
NJ = DFF // 128
INW = 2816
TB = 1024
NTT = TB // 128
ALPHA = (2 * DEPTH) ** 0.25
LN_EPS = 1e-5
GN_EPS = 1e-6
ROPE_THETA = 10000.0

ENGS = ("pe", "act", "dve", "pool", "sp")
DBG = os.environ.get("KDBG", "")


class _Stop(Exception):
    pass


def ck(name):
    if ("stop=" + name) in DBG:
        raise _Stop()


class T:
    __slots__ = ("name", "w", "r")

    def __init__(self, name):
        self.name = name
        self.w = None
        self.r = {}


class Ins:
    __slots__ = ("eng", "fn", "deps", "signal", "sem", "val", "is_dma", "know")

    def __init__(self, eng, fn, is_dma):
        self.eng = eng
        self.fn = fn
        self.deps = ()
        self.signal = False
        self.sem = None
        self.val = None
        self.is_dma = is_dma
        self.know = None


class DmaSem:
    __slots__ = ("name", "count", "h")

    def __init__(self, name):
        self.name = name
        self.count = 0
        self.h = None


class Prog:
    def __init__(self):
        self.q = {e: [] for e in ENGS}
        self.dma_sems = []

    def dma_sem(self, name):
        s = DmaSem(name)
        self.dma_sems.append(s)
        return s

    def add(self, eng, fn, reads=(), writes=(), dma_sem=None):
        ins = Ins(eng, fn, dma_sem is not None)
        if dma_sem is not None:
            dma_sem.count += 16
            ins.sem = dma_sem
            ins.val = dma_sem.count
        else:
            ins.sem = eng
        deps = {}
        for t in reads:
            d = t.w
            if d is not None and (d.is_dma or d.eng != eng or eng != "pe"):
                deps[id(d)] = d
        for t in writes:
            d = t.w
            if d is not None and (d.is_dma or d.eng != eng or eng != "pe"):
                deps[id(d)] = d
            for d in t.r.values():
                if d.is_dma or d.eng != eng:
                    deps[id(d)] = d
        ins.deps = tuple(deps.values())
        self.q[eng].append(ins)
        key = id(ins.sem) if ins.is_dma else eng
        for t in reads:
            t.r[key] = ins
        for t in writes:
            t.w = ins
            t.r = {}
        return ins

    def emit(self, nc, stack):
        for e in ENGS:
            for ins in self.q[e]:
                for d in ins.deps:
                    if not d.is_dma:
                        d.signal = True
        for e in ENGS:
            c = 0
            for ins in self.q[e]:
                if not ins.is_dma and ins.signal:
                    c += 1
                    ins.val = c
        esem = {e: stack.enter_context(nc.semaphore("s_" + e)) for e in ENGS}
        for s in self.dma_sems:
            s.h = stack.enter_context(nc.semaphore("d_" + s.name))

        def semh(i):
            return i.sem.h if i.is_dma else esem[i.sem]

        def semk(i):
            return id(i.sem) if i.is_dma else i.sem

        cur = {e: 0 for e in ENGS}
        known = {e: {} for e in ENGS}
        waits = {}
        progressed = True
        while progressed:
            progressed = False
            for e in ENGS:
                q = self.q[e]
                kn = known[e]
                while cur[e] < len(q):
                    ins = q[cur[e]]
                    ok = True
                    for d in ins.deps:
                        if d.know is None:
                            ok = False
                            break
                    if not ok:
                        break
                    wl = {}
                    for d in ins.deps:
                        k = semk(d)
                        if kn.get(k, 0) >= d.val:
                            continue
                        h = semh(d)
                        if wl.get(k, (None, 0))[1] < d.val:
                            wl[k] = (h, d.val)
                        for kk, vv in d.know.items():
                            if kn.get(kk, 0) < vv:
                                kn[kk] = vv
                        kn[k] = d.val
                    waits[id(ins)] = list(wl.values())
                    if ins.signal or ins.is_dma:
                        kk = dict(kn)
                        sk = semk(ins)
                        if kk.get(sk, 0) < ins.val:
                            kk[sk] = ins.val
                        ins.know = kk
                    else:
                        ins.know = kn
                    cur[e] += 1
                    progressed = True
        for e in ENGS:
            assert cur[e] == len(self.q[e]), "forward dependency on " + e

        blk = stack.enter_context(nc.Block())

        def run(eng_obj, e):
            for ins in self.q[e]:
                for h, v in waits[id(ins)]:
                    eng_obj.wait_ge(h, v)
                r = ins.fn(eng_obj)
                if ins.is_dma:
                    r.then_inc(ins.sem.h, 16)
                elif ins.signal:
                    r.then_inc(esem[e], 1)

        @blk.tensor
        def _(eng):
            run(eng, "pe")

        @blk.scalar
        def _(eng):
            run(eng, "act")

        @blk.vector
        def _(eng):
            run(eng, "dve")

        @blk.gpsimd
        def _(eng):
            run(eng, "pool")

        @blk.sync
        def _(eng):
            run(eng, "sp")


def _host_consts(S):
    nt = S // 128
    pos = np.arange(S, dtype=np.float32)
    inv_freq = (ROPE_THETA ** (-np.arange(0, HD, 2, dtype=np.float32) / HD)).astype(np.float32)
    ang = (pos[:, None] * inv_freq[None, :]).astype(np.float32)
    ropeA = np.concatenate([np.cos(ang), np.sin(ang)], axis=1).astype(np.float32)
    ang_freq = (1.0 / (ROPE_THETA ** np.linspace(0.0, 1.0, HD // 2, dtype=np.float32))).astype(np.float32)
    ang2 = (pos[:, None] * ang_freq[None, :]).astype(np.float32)
    ropeR = np.concatenate([np.cos(ang2), np.sin(ang2)], axis=1).astype(np.float32)
    ropeA = np.ascontiguousarray(ropeA.reshape(nt, 128, 64).transpose(1, 0, 2))
    ropeR = np.ascontiguousarray(ropeR.reshape(nt, 128, 64).transpose(1, 0, 2))
    H = 8
    lg = np.log1p(-np.exp2(-5.0 - np.arange(H, dtype=np.float64)))
    idx = np.arange(128, dtype=np.float64)
    scale = HD ** -0.5
    diff = idx[None, :] - idx[:, None]
    dt = np.where(diff[:, None, :] >= 0, np.exp(np.maximum(diff, 0.0)[:, None, :] * lg[None, :, None]), 0.0) * scale
    dt = dt.reshape(128, H * 128).astype(np.float32)
    wk = (np.exp((127.0 - idx)[:, None] * lg[None, :]) * scale).astype(np.float32)
    wqt = np.zeros((128, 4, 128), np.float64)
    gct = np.zeros((128, 4), np.float64)
    for c in range(4):
        for hp in range(2):
            h = 2 * c + hp
            wqt[hp * 64:(hp + 1) * 64, c, :] = np.exp((idx + 1.0) * lg[h])[None, :]
            gct[hp * 64:(hp + 1) * 64, c] = np.exp(128.0 * lg[h])
    wqt = wqt.reshape(128, 512).astype(np.float32)
    gct = gct.astype(np.float32)
    k = np.arange(128)[:, None]
    q = np.arange(128)[None, :]
    mprev = np.where(k > q, 0.0, -30000.0).astype(np.float32)
    mcur = np.where(k <= q, 0.0, -30000.0).astype(np.float32)
    masks = np.concatenate([mprev, mcur], axis=1).astype(np.float32)
    ident = np.eye(128, dtype=np.float32)
    return dict(ropeA=ropeA, ropeR=ropeR, dtab=dt, wk=wk, wqt=wqt, gct=gct, masks=masks, ident=ident)


def build_program(S=SEQ, NL=DEPTH):
    nblk = S // TB
    ntile = S // 128
    nc = bass.Bass("TRN2", target_bir_lowering=False)

    def din(name, shape, dt=F32):
        return nc.dram_tensor(name, list(shape), dt, kind="ExternalInput").ap()

    x_d = din("x", [S, D])
    f1gu_d = din("f1gu", [NL, D, 2 * DFF])
    f1d_d = din("f1d", [NL, DFF, D])
    f2gu_d = din("f2gu", [NL, D, 2 * DFF])
    f2d_d = din("f2d", [NL, DFF, D])
    win_d = din("win", [NL, D, INW])
    wout_d = din("wout", [NL, D, D])
    sinks_d = din("sinks", [1, NL * 8])
    lnp_d = din("lnp", [NL * 6, D])
    ropeA_d = din("ropeA", [128, ntile, 64])
    ropeR_d = din("ropeR", [128, ntile, 64])
    dtab_d = din("dtab", [128, 1024])
    wk_d = din("wk", [128, 8])
    wqt_d = din("wqt", [128, 512])
    gct_d = din("gct", [128, 4])
    masks_d = din("masks", [128, 256])
    ident_d = din("ident", [128, 128])
    out_d = nc.dram_tensor("out", [S, D], F32, kind="ExternalOutput").ap()

    def dscr(name, shape):
        return nc.dram_tensor(name, list(shape), BF16).ap()

    NG = NJ // 2
    wgu_s = dscr("wgu_s", [NL * 2 * NG, 128, 8 * 512])
    wd_s = dscr("wd_s", [NL * 2 * 2, 128, NJ * 512])
    win_s = dscr("win_s", [NL * 6, 128, 8 * 512])
    wout_s = dscr("wout_s", [NL, 128, 8 * 1024])

    gu_d = (f1gu_d, f2gu_d)
    dn_d = (f1d_d, f2d_d)

    IG = [(0, 512), (512, 256), (768, 512), (1280, 512), (1792, 512), (2304, 512)]

    p = Prog()
    with ExitStack() as st:
        def sb(name, shape, dt):
            return st.enter_context(nc.sbuf_tensor(name, list(shape), dt))

        X = sb("X", [128, NTT, D], F32)
        XT = sb("XT", [128, 8, TB], BF16)
        BIG = sb("BIG", [128, NJ * TB], BF16)
        NWU = 2
        WU = [sb("WU%d" % i, [128, 8, 512], BF16) for i in range(NWU)]
        WD = [sb("WD%d" % i, [128, NJ * 512], BF16) for i in range(2)]
        GB = sb("GB", [128, 2, D], F32)
        SCR = [sb("SCR%d" % i, [128, 512], F32) for i in range(2)]
        XB = SCR[1][:].bitcast(BF16)
        QAZ = [sb("QAZ%d" % i, [128, 4, 128], BF16) for i in range(2)]
        KD = [sb("KD%d" % i, [128, 2, 128], BF16) for i in range(2)]
        QRZ = [sb("QRZ%d" % i, [128, 4, 128], BF16) for i in range(2)]
        KRT = sb("KRT", [128, 4, 128], BF16)
        PTL = [[sb("PTL%d%d" % (a, b), [128, 512], BF16) for b in range(2)] for a in range(2)]
        SRB = sb("SRB", [128, 8, 128], BF16)
        QW = [sb("QW%d" % i, [128, 4, 128], BF16) for i in range(2)]
        VW = sb("VW", [128, 8, 64], BF16)
        YCAT = sb("YCAT", [128, D], BF16)
        YCT = sb("YCT", [128, 8, 128], BF16)
        TABA = sb("TABA", [128, NTT, 64], F32)
        TABR = sb("TABR", [128, NTT, 64], F32)
        DTAB = sb("DTAB", [128, 1024], F32)
        WQT = sb("WQT", [128, 512], F32)
        GCT = sb("GCT", [128, 4], F32)
        WKB = sb("WKB", [128, 8], F32)
        MASK = sb("MASK", [128, 2, 128], BF16)
        IDENT = sb("IDENT", [128, 128], BF16)
        ES = sb("ES", [128, NL * 8], F32)
        STt = [sb("ST%d" % l, [128, 256], F32) for l in range(NL)]
        SBt = [sb("SB%d" % l, [128, 4, 64], BF16) for l in range(NL)]
        CKT = [sb("CKT%d" % l, [128, 2, 128], BF16) for l in range(NL)]
        CV = [sb("CV%d" % l, [128, 2, 65], BF16) for l in range(NL)]
        VA1 = sb("VA1", [128, NTT, 2, 65], BF16)
        STATS = sb("STATS", [128, 2, 6], F32)
        MV = sb("MV", [128, 8], F32)
        GN = sb("GN", [128, 6, 8], F32)
        DEN = sb("DEN", [128, 2, 8], F32)
        EPS = sb("EPS", [128, 2], F32)
        MARK = sb("MARK", [128, 2], F32)

        PA = st.enter_context(nc.psum_tensor("PA", [128, 1024], F32))
        PB = st.enter_context(nc.psum_tensor("PB", [128, 1024], F32))
        PC = st.enter_context(nc.psum_tensor("PC", [128, 1024], F32))
        PT = st.enter_context(nc.psum_tensor("PT", [128, 2048], BF16))
        PP = [PA, PB, PC]

        ACTT = BIG[:].rearrange("p (j t) -> p j t", j=NJ)
        Hh = BIG[:].rearrange("p (t c) -> p t c", t=NTT)
        H_QA, H_KD, H_QR, H_KR, H_VR, H_SG = 0, 512, 768, 1280, 1792, 2304

        tX = [T("X%d" % i) for i in range(NTT)]
        tXT = [T("XT%d" % i) for i in range(NTT)]
        tBIG = T("BIG")
        tACT = [[T("ACT%d_%d" % (j, h)) for h in range(2)] for j in range(NJ)]
        tH = [{k: T("H%d%s" % (i, k)) for k in ("qa", "kd", "qr", "kr", "vr", "sg")} for i in range(NTT)]
        tVA1 = [T("VA1_%d" % i) for i in range(NTT)]
        tWU = [T("WU%d" % i) for i in range(NWU)]
        tWD = [[T("WD%d_%d" % (i, k)) for k in range(NG)] for i in range(2)]
        tG = T("G")
        tBv = T("B")
        tSCR = [T("SCR%d" % i) for i in range(2)]
        tQAZ = T("QAZ")
        tKD = [T("KD%d" % i) for i in range(2)]
        tQRZ = T("QRZ")
        tKRT = T("KRT")
        tPTL = [[T("PTL%d%d" % (a, b)) for b in range(2)] for a in range(2)]
        tSRB = T("SRB")
        tQW = T("QW")
        tVW = T("VW")
        tYC = [T("YCa"), T("YCr")]
        tYCT = T("YCT")
        tTABA = T("TABA")
        tTABR = T("TABR")
        tDTAB = T("cDTAB")
        tWQT = T("cWQT")
        tGCT = T("cGCT")
        tWKB = T("cWKB")
        tESr = T("cESr")
        tMASK = T("cMASK")
        tIDENT = T("cIDENT")
        tEPS = T("cEPS")
        tES = T("ES")
        tST = [T("ST%d" % l) for l in range(NL)]
        tSB = [T("SB%d" % l) for l in range(NL)]
        tCKT = [T("CKT%d" % l) for l in range(NL)]
        tCV = [T("CV%d" % l) for l in range(NL)]
        tSTATS = T("STATS")
        tMV = T("MV")
        tGN = T("GN")
        tGN2 = T("GN2")
        tDEN = T("DEN")
        tMARK = T("MARK")
        tP = [[T("P%d_%d" % (i, h)) for h in range(2)] for i in range(3)]
        tPT = [T("PT0"), T("PT1")]
        tOUT = [T("OUT%d" % i) for i in range(NTT)]

        sWU = [p.dma_sem("wu%d" % i) for i in range(NWU)]
        sWD = [[p.dma_sem("wd%d_%d" % (i, k)) for k in range(NG)] for i in range(2)]
        sG = p.dma_sem("g")
        sB = p.dma_sem("b")
        sX = [p.dma_sem("x%d" % i) for i in range(NTT)]
        sTA = p.dma_sem("ta")
        sTR = p.dma_sem("tr")

        def mm(out, lhsT, rhs, start, stop, r, w):
            p.add("pe", lambda e: e.matmul(out, lhsT=lhsT, rhs=rhs, start=start, stop=stop), r, w)

        def tr(out, in_, r, w):
            p.add("pe", lambda e: e.transpose(out, in_, IDENT[:]), list(r) + [tIDENT], w)

        def act(out, in_, func, r, w, bias=None, scale=None):
            kw = {}
            if bias is not None:
                kw["bias"] = bias
            if scale is not None:
                kw["scale"] = scale
            p.add("act", lambda e: e.activation(out=out, in_=in_, func=func, **kw), r, w)

        def tt(eng, out, in0, in1, op, r, w):
            p.add(eng, lambda e: e.tensor_tensor(out=out, in0=in0, in1=in1, op=op), r, w)

        def ts(eng, out, in0, s1, s2, op0, op1, r, w):
            if s2 is None:
                p.add(eng, lambda e: e.tensor_scalar(out=out, in0=in0, scalar1=s1, scalar2=None, op0=op0), r, w)
            else:
                p.add(eng, lambda e: e.tensor_scalar(out=out, in0=in0, scalar1=s1, scalar2=s2, op0=op0, op1=op1), r, w)

        def stt(eng, out, in0, scalar, in1, op0, op1, r, w):
            p.add(eng, lambda e: e.scalar_tensor_tensor(out=out, in0=in0, scalar=scalar, in1=in1, op0=op0, op1=op1), r, w)

        def cp(eng, out, in_, r, w):
            if eng == "act":
                p.add("act", lambda e: e.copy(out=out, in_=in_), r, w)
            else:
                p.add(eng, lambda e: e.tensor_copy(out=out, in_=in_), r, w)

        def red(eng, out, in_, r, w):
            p.add(eng, lambda e: e.tensor_reduce(out=out, in_=in_, axis=AX.X, op=ALU.add), r, w)

        def dma(eng, out, in_, sem, r, w):
            p.add(eng, lambda e: e.dma_start(out=out, in_=in_), r, w, dma_sem=sem)

        def memset(eng, ap, val, w):
            p.add(eng, lambda e: e.memset(ap, val), (), w)

        tWS = {}
        sWS = {}
        tWSg = {}
        NTHR = 6
        thr = [T("thr%d" % i) for i in range(NTHR)]
        thr_i = [0]

        def cdma(out, in_, sem, t, kind=""):
            if ("skip" + kind) in DBG:
                return
            dma("pool", out, in_, sem, (), [t])

        def cast_jobs(l):
            jobs = []

            def ffn_jobs(f):
                key = (l, "gu", f)
                sWS[key] = p.dma_sem("cgu%d_%d" % (l, f))
                tWS[key] = []
                src = gu_d[f][l].rearrange("(k q) c -> q k c", q=128)
                for g in range(NG):
                    dst = wgu_s[(l * 2 + f) * NG + g].rearrange("q (k c) -> q k c", k=8)
                    fine = (l == 0 and f == 0)
                    gsem = p.dma_sem("cg0_%d" % g) if fine else sWS[key]
                    if fine:
                        tWSg[g] = []
                    for au in range(2):
                        t = T("ws")
                        (tWSg[g] if fine else tWS[key]).append(t)
                        jobs.append(lambda dst=dst, src=src, au=au, g=g, gsem=gsem, t=t: cdma(
                            dst[:, :, au * 256:(au + 1) * 256],
                            src[:, :, au * DFF + g * 256: au * DFF + (g + 1) * 256], gsem, t, "gu"))
                key = (l, "dn", f)
                sWS[key] = p.dma_sem("cdn%d_%d" % (l, f))
                tWS[key] = []
                src = dn_d[f][l].rearrange("(j q) c -> q j c", q=128)
                for dh in range(2):
                    dst = wd_s[(l * 2 + f) * 2 + dh].rearrange("q (j c) -> q j c", j=NJ)
                    for jh in range(2):
                        t = T("ws")
                        tWS[key].append(t)
                        jobs.append(lambda dst=dst, src=src, jh=jh, dh=dh, key=key, t=t: cdma(
                            dst[:, jh * 11:(jh + 1) * 11, :],
                            src[:, jh * 11:(jh + 1) * 11, dh * 512:(dh + 1) * 512], sWS[key], t, "dn"))

            ffn_jobs(0)
            key = (l, "in", 0)
            sWS[key] = p.dma_sem("cin%d" % l)
            tWS[key] = []
            src = win_d[l].rearrange("(k q) c -> q k c", q=128)
            for gi, (c0, wd_) in enumerate(IG):
                dst = win_s[l * 6 + gi].rearrange("q (k c) -> q k c", k=8)
                t = T("ws")
                tWS[key].append(t)
                jobs.append(lambda dst=dst, src=src, c0=c0, wd_=wd_, key=key, t=t: cdma(
                    dst[:, :, 0:wd_], src[:, :, c0:c0 + wd_], sWS[key], t, "in"))
            key = (l, "out", 0)
            sWS[key] = p.dma_sem("cout%d" % l)
            tWS[key] = []
            src = wout_d[l].rearrange("(k q) c -> q k c", q=128)
            dst = wout_s[l].rearrange("q (k c) -> q k c", k=8)
            for hh in range(2):
                t = T("ws")
                tWS[key].append(t)
                jobs.append(lambda dst=dst, src=src, hh=hh, key=key, t=t: cdma(
                    dst[:, hh * 4:(hh + 1) * 4, :], src[:, hh * 4:(hh + 1) * 4, :], sWS[key], t, "out"))
            ffn_jobs(1)
            return jobs

        pending_casts = []

        def cast_layer(l):
            for j in cast_jobs(l):
                j()

        def cast_some(n):
            for _ in range(n):
                if pending_casts:
                    pending_casts.pop(0)()

        sCs = [p.dma_sem("c%d" % i) for i in range(7)]
        dma("sp", DTAB[:], dtab_d, sCs[0], (), [tDTAB])
        dma("sp", WQT[:], wqt_d, sCs[1], (), [tWQT])
        dma("sp", GCT[:], gct_d, sCs[2], (), [tGCT])
        dma("sp", WKB[:], wk_d, sCs[3], (), [tWKB])
        dma("sp", ES[:], sinks_d.partition_broadcast(128), sCs[4], (), [tESr])
        dma("pool", MASK[:].rearrange("p b t -> p (b t)"), masks_d, sCs[5], (), [tMASK])
        dma("pool", IDENT[:], ident_d, sCs[6], (), [tIDENT])
        cast_layer(0)
        act(ES[:], ES[:], AF.Exp, [tESr], [tES, tESr])
        memset("dve", EPS[:, 0:1], LN_EPS / (ALPHA * ALPHA), [tEPS])
        memset("dve", EPS[:, 1:2], GN_EPS, [tEPS])
        memset("dve", VA1[:, :, :, 64:65], 1.0, tVA1)
        for l in range(NL):
            memset("dve", STt[l][:], 0.0, [tST[l]])
            memset("dve", SBt[l][:], 0.0, [tSB[l]])
        memset("dve", MARK[:], 0.0, [tMARK])
        for i in range(2):
            memset("dve", QAZ[i][:], 0.0, [tQAZ])
            memset("dve", QRZ[i][:], 0.0, [tQRZ])

        pp_i = [0]
        tail = []

        def flush_tail():
            while tail:
                tail.pop(0)()

        wu_i = [0]

        def next_pair():
            i = pp_i[0] % 3
            pp_i[0] += 1
            return i

        def make_xt(tti, b=None):
            if b is None:
                b = tti % 2
            cp("act", XB, X[:, tti, :], [tX[tti]], [tSCR[1]])
            for k in range(8):
                tr(PT[:, b * 1024 + k * 128: b * 1024 + (k + 1) * 128], XB[:, k * 128:(k + 1) * 128], [tSCR[1]], [tPT[b]])
            cp("act", XT[:, :, tti * 128:(tti + 1) * 128],
               PT[:, b * 1024:(b + 1) * 1024].rearrange("p (k t) -> p k t", k=8), [tPT[b]], [tXT[tti]])

        def ln_B(tti, srcs, cscale):
            ln_B1(tti, srcs, cscale)
            ln_B2(tti)

        def ln_B1(tti, srcs, cscale):
            for ap, tl, c0, n in srcs:
                stt("dve", X[:, tti, c0:c0 + n], ap, cscale, X[:, tti, c0:c0 + n], ALU.mult, ALU.add,
                    list(tl) + [tX[tti]], [tX[tti]])

        def ln_B2(tti):
            ln_B2a(tti)
            ln_B2b(tti)

        def ln_B2a(tti):
            xs = X[:, tti, :]
            p.add("dve", lambda e: e.bn_stats(out=STATS[:, 0, :], in_=X[:, tti, 0:512]), [tX[tti]], [tSTATS])
            p.add("dve", lambda e: e.bn_stats(out=STATS[:, 1, :], in_=X[:, tti, 512:1024]), [tX[tti]], [tSTATS])
            p.add("dve", lambda e: e.bn_aggr(out=MV[:, 0:2], in_=STATS[:]), [tSTATS], [tMV])
            act(MV[:, 2:3], MV[:, 1:2], AF.Ln, [tMV, tEPS], [tMV], bias=EPS[:, 0:1], scale=1.0)
            act(MV[:, 2:3], MV[:, 2:3], AF.Exp, [tMV], [tMV], scale=-0.5)
            stt("dve", MV[:, 3:4], MV[:, 0:1], -1.0, MV[:, 2:3], ALU.mult, ALU.mult, [tMV], [tMV])
            act(xs, xs, AF.Identity, [tX[tti], tMV], [tX[tti]], bias=MV[:, 3:4], scale=MV[:, 2:3])

        def ln_B2b(tti):
            xs = X[:, tti, :]
            tt("pool", xs, xs, GB[:, 0, :], ALU.mult, [tX[tti], tG], [tX[tti]])
            tt("pool", xs, xs, GB[:, 1, :], ALU.add, [tX[tti], tBv], [tX[tti]])

        def ln_C(tti, last, row0, b=None):
            if last:
                dma("sp", out_d[row0 + tti * 128: row0 + (tti + 1) * 128, :], X[:, tti, :], sX[tti], [tX[tti]], [tOUT[tti]])
            else:
                make_xt(tti, b)

        def load_ln(l, which):
            dma("sp", GB[:, 0, :], lnp_d[l * 6 + 2 * which: l * 6 + 2 * which + 1, :].partition_broadcast(128),
                sG, (), [tG])
            dma("sp", GB[:, 1, :], lnp_d[l * 6 + 2 * which + 1: l * 6 + 2 * which + 2, :].partition_broadcast(128),
                sB, (), [tBv])

        def ffn(l, f, last, row0):
            p.add("dve", lambda e: e.memset(MARK[:, 0:1], 0.0), [], [tBIG, tMARK])
            kgu = (l, "gu", f)
            kdn = (l, "dn", f)
            for g in range(NG):
                cast_some(2)
                s = wu_i[0] % NWU
                wu_i[0] += 1
                dma("sp", WU[s][:].rearrange("p k c -> p (k c)"), wgu_s[(l * 2 + f) * NG + g], sWU[s],
                    tWSg[g] if (l == 0 and f == 0) else tWS[kgu], [tWU[s]])
                if g == 2:
                    load_ln(l, 0 if f == 0 else 2)
                for dh in range(2):
                    dma("sp", WD[dh][:, g * 1024:(g + 1) * 1024], wd_s[(l * 2 + f) * 2 + dh][:, g * 1024:(g + 1) * 1024],
                        sWD[dh][g], tWS[kdn], [tWD[dh][g]])
                for half in range(2):
                    if half == 1:
                        flush_tail()
                    xr = [tXT[half * 4 + i] for i in range(4)]
                    for jj in range(2):
                        j = 2 * g + jj
                        pi = next_pair()
                        for au in range(2):
                            for k in range(8):
                                mm(PP[pi][:, au * 512:(au + 1) * 512],
                                   WU[s][:, k, au * 256 + jj * 128: au * 256 + (jj + 1) * 128],
                                   XT[:, k, half * 512:(half + 1) * 512], k == 0, k == 7,
                                   [tWU[s]] + xr, [tP[pi][au]])
                        sc = (2 * g + jj + half) % 2
                        act(SCR[sc][:], PP[pi][:, 0:512], AF.Silu, [tP[pi][0]], [tSCR[sc]])
                        tt("dve", ACTT[:, j, half * 512:(half + 1) * 512], SCR[sc][:], PP[pi][:, 512:1024], ALU.mult,
                           [tSCR[sc], tP[pi][1], tBIG], [tACT[j][half]])
            ck("up")
            pis = {}
            for step in range(NTT + 2):
                if step < NTT:
                    tti = step
                    pi = next_pair()
                    pis[tti] = pi
                    half = tti // 4
                    for j in range(NJ):
                        for dh in range(2):
                            mm(PP[pi][:, dh * 512:(dh + 1) * 512], ACTT[:, j, tti * 128:(tti + 1) * 128],
                               WD[dh][:, j * 512:(j + 1) * 512], j == 0, j == NJ - 1,
                               [tACT[j][half], tWD[dh][j // 2], tBIG], [tP[pi][dh]])
                if 0 <= step - 1 < NTT:
                    t1 = step - 1
                    ln_B(t1, [(PP[pis[t1]][:], [tP[pis[t1]][0], tP[pis[t1]][1]], 0, 1024)], 0.5 / ALPHA)
                if 0 <= step - 2 < NTT:
                    if step - 2 >= NTT - 2 and not last:
                        tail.append(lambda t=step - 2: ln_C(t, False, row0))
                    else:
                        ln_C(step - 2, last, row0)

        def f32v(ap):
            return ap.bitcast(F32)

        ROPE_SETS = [
            [(f32v(PTL[0][0][:]), tPTL[0][0]), (f32v(PTL[0][1][:]), tPTL[0][1]),
             (f32v(PTL[1][0][:]), tPTL[1][0]), (f32v(PTL[1][1][:]), tPTL[1][1])],
            [(f32v(YCAT[:, 0:512]), tYC[0]), (f32v(YCAT[:, 512:1024]), tYC[1]),
             (f32v(VW[:].rearrange("p h e -> p (h e)")), tVW), (f32v(SRB[:, 0:4, :].rearrange("p h t -> p (h t)")), tSRB)],
        ]
        rope_i = [0]

        def rope_tmps(nh):
            st_ = ROPE_SETS[rope_i[0] % 2]
            rope_i[0] += 1
            n = nh * 32
            return [(ap[:, 0:n].rearrange("p (h f) -> p h f", h=nh), tl) for ap, tl in st_]

        def rope_half(psrc, nh, dst_list, tab, tti, rd, wr):
            P4 = psrc.rearrange("p (h t f) -> p h t f", h=nh, t=2)
            x1 = P4[:, :, 0, :]
            x2 = P4[:, :, 1, :]
            Cb = tab[:, tti, 0:32].unsqueeze(1).to_broadcast([128, nh, 32])
            Sb = tab[:, tti, 32:64].unsqueeze(1).to_broadcast([128, nh, 32])
            (A, tA), (B, tB), (A2, tA2), (B2, tB2) = rope_tmps(nh)
            tt("dve", A, x1, Cb, ALU.mult, rd + [tTABA], [tA])
            tt("dve", B, x2, Sb, ALU.mult, rd + [tTABA], [tB])
            tt("dve", A2, x2, Cb, ALU.mult, rd + [tTABA], [tA2])
            tt("dve", B2, x1, Sb, ALU.mult, rd + [tTABA], [tB2])
            for dst in dst_list:
                tt("pool", dst[:, :, 0, :], A, B, ALU.subtract, [tA, tB, tBIG], wr)
                tt("pool", dst[:, :, 1, :], A2, B2, ALU.add, [tA2, tB2, tBIG], wr)

        def rope_pair(psrc, dst, tab, tti, rd, wr):
            P4 = psrc.rearrange("p (h f t) -> p h f t", h=8, t=2)
            D4 = dst.rearrange("p (h f t) -> p h f t", h=8, t=2)
            xe = P4[:, :, :, 0]
            xo = P4[:, :, :, 1]
            Cb = tab[:, tti, 0:32].unsqueeze(1).to_broadcast([128, 8, 32])
            Sb = tab[:, tti, 32:64].unsqueeze(1).to_broadcast([128, 8, 32])
            (A, tA), (B, tB), (A2, tA2), (B2, tB2) = rope_tmps(8)
            tt("dve", A, xe, Cb, ALU.mult, rd + [tTABR], [tA])
            tt("dve", B, xo, Sb, ALU.mult, rd + [tTABR], [tB])
            tt("dve", A2, xo, Cb, ALU.mult, rd + [tTABR], [tA2])
            tt("dve", B2, xe, Sb, ALU.mult, rd + [tTABR], [tB2])
            tt("pool", D4[:, :, :, 0], A, B, ALU.subtract, [tA, tB, tBIG], wr)
            tt("pool", D4[:, :, :, 1], A2, B2, ALU.add, [tA2, tB2, tBIG], wr)

        def mixer(l, tb, row0):
            p.add("dve", lambda e: e.memset(MARK[:, 1:2], 0.0), [], [tBIG, tMARK])
            kin = (l, "in", 0)
            first_blk = (tb == 0)
            hb_i = 0
            for gi, (c0, wd_) in enumerate(IG):
                cast_some(2)
                s = wu_i[0] % NWU
                wu_i[0] += 1
                dma("sp", WU[s][:].rearrange("p k c -> p (k c)"), win_s[l * 6 + gi], sWU[s], tWS[kin], [tWU[s]])
                if gi == 2:
                    load_ln(l, 1)
                if 1 <= gi <= 4:
                    q4 = gi - 1
                    dma("sp", WD[0][:, q4 * 2048:(q4 + 1) * 2048], wout_s[l][:, q4 * 2048:(q4 + 1) * 2048],
                        sWD[0][2 * q4], tWS[(l, "out", 0)], [tWD[0][2 * q4], tWD[0][2 * q4 + 1]])
                for tti in range(NTT):
                    if tti == NTT - 2:
                        flush_tail()
                    pi, hb = (hb_i // 2) % 3, hb_i % 2
                    hb_i += 1
                    ps = PP[pi][:, hb * 512: hb * 512 + wd_]
                    tp = tP[pi][hb]
                    for k in range(8):
                        mm(ps, XT[:, k, tti * 128:(tti + 1) * 128], WU[s][:, k, 0:wd_], k == 0, k == 7,
                           [tWU[s], tXT[tti]], [tp])
                    hrow = Hh[:, tti, :]
                    if gi == 0:
                        rope_half(ps, 8, [hrow[:, H_QA:H_QA + 512].rearrange("p (h t f) -> p h t f", h=8, t=2)],
                                  TABA, tti, [tp], [tH[tti]["qa"]])
                    elif gi == 1:
                        kd = hrow[:, H_KD:H_KD + 256].rearrange("p (v r t f) -> p v r t f", v=2, r=2, t=2)
                        rope_half(ps[:, 0:128], 2, [kd[:, :, 0, :, :], kd[:, :, 1, :, :]], TABA, tti, [tp], [tH[tti]["kd"]])
                        cp("act", VA1[:, tti, :, 0:64], ps[:, 128:256].rearrange("p (v e) -> p v e", v=2), [tp], [tVA1[tti]])
                    elif gi == 2:
                        rope_pair(ps, hrow[:, H_QR:H_QR + 512], TABR, tti, [tp], [tH[tti]["qr"]])
                    elif gi == 3:
                        rope_pair(ps, hrow[:, H_KR:H_KR + 512], TABR, tti, [tp], [tH[tti]["kr"]])
                    elif gi == 4:
                        cp("act", hrow[:, H_VR:H_VR + 512], ps, [tp, tBIG], [tH[tti]["vr"]])
                    else:
                        act(hrow[:, H_SG:H_SG + 512], ps, AF.Silu, [tp, tBIG], [tH[tti]["sg"]])
            ck("inproj")
            def kv_prev(tti):
                qs = tti % 2
                if tti > 0:
                    return (lambda v: KD[1 - qs][:, v, :]), tKD[1 - qs], (lambda v: VA1[:, tti - 1, v, :]), tVA1[tti - 1]
                return (lambda v: CKT[l][:, v, :]), tCKT[l], (lambda v: CV[l][:, v, :]), tCV[l]

            def front_tr(tti):
                hrow = Hh[:, tti, :]
                th = tH[tti]
                qs = tti % 2
                for c in range(8):
                    col = H_QR + c * 128 if c < 4 else H_KR + (c - 4) * 128
                    tr(PT[:, 1024 + c * 128: 1024 + (c + 1) * 128], hrow[:, col:col + 128],
                       [th["qr"] if c < 4 else th["kr"], tBIG], [tPT[1]])
                ptr = PT[:, 1024:2048].rearrange("p (c t) -> p c t", c=8)
                cp("dve", QRZ[0][0:64, :, :], ptr[0:64, 0:4, :], [tPT[1]], [tQRZ])
                cp("dve", QRZ[1][64:128, :, :], ptr[64:128, 0:4, :], [tPT[1]], [tQRZ])
                cp("dve", KRT[:], ptr[:, 4:8, :], [tPT[1]], [tKRT])
                for par in range(2):
                    tt("pool", QW[par][:].rearrange("p c t -> p (c t)"), QRZ[par][:].rearrange("p c t -> p (c t)"), WQT[:], ALU.mult,
                       [tQRZ, tWQT], [tQW])
                for c in range(6):
                    col = H_QA + c * 128 if c < 4 else H_KD + (c - 4) * 128
                    tr(PT[:, c * 128:(c + 1) * 128], hrow[:, col:col + 128], [th["qa"] if c < 4 else th["kd"], tBIG], [tPT[0]])
                pta = PT[:, 0:768].rearrange("p (c t) -> p c t", c=6)
                cp("act", QAZ[0][0:64, :, :], pta[0:64, 0:4, :], [tPT[0]], [tQAZ])
                cp("act", QAZ[1][64:128, :, :], pta[64:128, 0:4, :], [tPT[0]], [tQAZ])
                cp("act", KD[qs][:], pta[:, 4:6, :], [tPT[0]], [tKD[qs]])

            def front(tti):
                gt = tb * NTT + tti
                hrow = Hh[:, tti, :]
                th = tH[tti]
                qs = tti % 2
                has_prev = gt > 0
                kprev, tkprev, vprev, tvprev = kv_prev(tti)
                blks = ([0] if has_prev else []) + [1]
                for h in range(8):
                    c, par = h // 2, h % 2
                    mm(PC[:, h * 128:(h + 1) * 128], KRT[:, c, :], QRZ[par][:, c, :],
                       True, True, [tKRT, tQRZ], [tP[2][h // 4]])
                tt("dve", SRB[:].rearrange("p h t -> p (h t)"), PC[:], DTAB[:], ALU.mult, [tP[2][0], tP[2][1], tDTAB], [tSRB])
                tt("pool", VW[:], hrow[:, H_VR:H_VR + 512].rearrange("p (h e) -> p h e", h=8),
                   WKB[:].unsqueeze(2).to_broadcast([128, 8, 64]), ALU.mult, [th["vr"], tBIG, tWKB], [tVW])
                for kvh in range(2):
                    for blk in blks:
                        for slot in range(4):
                            cc, par = slot // 2, slot % 2
                            chunk = kvh * 2 + cc
                            kT = kprev(kvh) if blk == 0 else KD[qs][:, kvh, :]
                            o_ap = PP[kvh][:, blk * 512 + slot * 128: blk * 512 + (slot + 1) * 128]
                            mm(o_ap, kT, QAZ[par][:, chunk, :], slot == 0, False,
                               [tkprev if blk == 0 else tKD[qs], tQAZ], [tP[kvh][blk]])
                            mm(o_ap, IDENT[:], MASK[:, blk, :], False, slot == 3, [tIDENT, tMASK], [tP[kvh][blk]])
                for kvh in range(2):
                    for blk in blks:
                        act(PTL[kvh][blk][:], PP[kvh][:, blk * 512:(blk + 1) * 512], AF.Exp, [tP[kvh][blk]], [tPTL[kvh][blk]],
                            scale=HD ** -0.5)
                for h in range(8):
                    c, par = h // 2, h % 2
                    mm(PC[:, h * 64:(h + 1) * 64], SRB[:, h, :], hrow[:, H_VR + h * 64: H_VR + (h + 1) * 64], True, False,
                       [tSRB, th["vr"], tBIG], [tP[2][0]])
                    mm(PC[:, h * 64:(h + 1) * 64], QW[par][:, c, :], SBt[l][:, c, :], False, True,
                       [tQW, tSB[l]], [tP[2][0]])
                for c in range(4):
                    mm(PC[:, 512 + c * 128: 512 + (c + 1) * 128], hrow[:, H_KR + c * 128: H_KR + (c + 1) * 128],
                       VW[:, 2 * c:2 * c + 2, :].rearrange("p h e -> p (h e)"), True, True, [th["kr"], tBIG, tVW], [tP[2][1]])
                for kvh in range(2):
                    for slot in range(4):
                        for bi, blk in enumerate(blks):
                            v1 = vprev(kvh) if blk == 0 else VA1[:, tti, kvh, :]
                            mm(PP[kvh][:, slot * 65:(slot + 1) * 65],
                               PTL[kvh][blk][:, slot * 128:(slot + 1) * 128], v1, bi == 0, bi == len(blks) - 1,
                               [tPTL[kvh][blk], tvprev if blk == 0 else tVA1[tti]], [tP[kvh][0]])
                pr3 = PC[:, 0:512].rearrange("p (h e) -> p h e", h=8)
                red("dve", GN[:, 0, :], pr3, [tP[2][0]], [tGN])
                act(SCR[0][:], PC[:, 0:512], AF.Square, [tP[2][0], tGN], [tSCR[0]])
                red("dve", GN[:, 1, :], SCR[0][:].rearrange("p (h e) -> p h e", h=8), [tSCR[0]], [tGN])
                ts("dve", GN[:, 2, :], GN[:, 0, :], 1.0 / 64, None, ALU.mult, None, [tGN], [tGN])
                tt("dve", GN[:, 5, :], GN[:, 2, :], GN[:, 2, :], ALU.mult, [tGN], [tGN])
                stt("dve", GN[:, 3, :], GN[:, 1, :], 1.0 / 64, GN[:, 5, :], ALU.mult, ALU.subtract, [tGN], [tGN])
                act(GN[:, 3, :], GN[:, 3, :], AF.Ln, [tGN, tEPS], [tGN], bias=EPS[:, 1:2], scale=1.0)
                act(GN[:, 4, :], GN[:, 3, :], AF.Exp, [tGN], [tGN2], scale=-0.5)
                t1 = SCR[0][:].rearrange("p (h e) -> p h e", h=8)
                tt("dve", t1, pr3, GN[:, 2, :].unsqueeze(2).to_broadcast([128, 8, 64]), ALU.subtract, [tP[2][0], tGN], [tSCR[0]])
                tt("dve", SCR[0][:], SCR[0][:], hrow[:, H_SG:H_SG + 512], ALU.mult, [tSCR[0], th["sg"], tBIG], [tSCR[0]])
                for kvh in range(2):
                    po = PP[kvh][:, 0:260].rearrange("p (s e) -> p s e", s=4)
                    tt("dve", DEN[:, 0, kvh * 4:(kvh + 1) * 4], po[:, :, 64], ES[:, l * 8 + kvh * 4: l * 8 + kvh * 4 + 4], ALU.add,
                       [tP[kvh][0], tES], [tDEN])
                    p.add("dve", lambda e, kvh=kvh: e.reciprocal(out=DEN[:, 1, kvh * 4:(kvh + 1) * 4], in_=DEN[:, 0, kvh * 4:(kvh + 1) * 4]),
                          [tDEN], [tDEN])
                tt("dve", YCAT[:, 512:1024].rearrange("p (h e) -> p h e", h=8), t1,
                   GN[:, 4, :].unsqueeze(2).to_broadcast([128, 8, 64]), ALU.mult, [tSCR[0], tGN2], [tYC[1]])
                for kvh in range(2):
                    po = PP[kvh][:, 0:260].rearrange("p (s e) -> p s e", s=4)
                    tt("dve", YCAT[:, kvh * 256:(kvh + 1) * 256].rearrange("p (s e) -> p s e", s=4), po[:, :, 0:64],
                       DEN[:, 1, kvh * 4:(kvh + 1) * 4].unsqueeze(2).to_broadcast([128, 4, 64]), ALU.mult,
                       [tP[kvh][0], tDEN], [tYC[0]])
                if tti == NTT - 1 and tb < nblk - 1:
                    cp("pool", CKT[l][:], KD[qs][:], [tKD[qs]], [tCKT[l]])
                    cp("pool", CV[l][:], VA1[:, tti, :, :], [tVA1[tti]], [tCV[l]])
                pkv = PC[:, 512:1024].rearrange("p (c m) -> p c m", c=4)
                st3 = STt[l][:].rearrange("p (c e) -> p c e", c=4)
                tt("pool", st3, st3, GCT[:].unsqueeze(2).to_broadcast([128, 4, 64]), ALU.mult, [tST[l], tGCT], [tST[l]])
                tt("dve", st3[0:64], st3[0:64], pkv[0:64, :, 0:64], ALU.add, [tST[l], tP[2][1]], [tST[l]])
                tt("dve", st3[64:128], st3[64:128], pkv[64:128, :, 64:128], ALU.add, [tST[l], tP[2][1]], [tST[l]])
                cp("pool", SBt[l][:], st3, [tST[l]], [tSB[l]])

            def out_stage(tti):
                for c in range(8):
                    tr(PT[:, c * 128:(c + 1) * 128], YCAT[:, c * 128:(c + 1) * 128], [tYC[0], tYC[1]], [tPT[0]])
                cp("act", YCT[:], PT[:, 0:1024].rearrange("p (c t) -> p c t", c=8), [tPT[0]], [tYCT])
                for dh in range(2):
                    for c in range(8):
                        mm(PP[dh][:, 512:1024], YCT[:, c, :], WD[0][:, c * 1024 + dh * 512: c * 1024 + (dh + 1) * 512],
                           c == 0, c == 7, [tYCT, tWD[0][c]], [tP[dh][1]])

            front_tr(0)
            for step in range(NTT + 2):
                if step < NTT:
                    front(step)
                    ck("f_%d" % step)
                if 0 <= step - 2 < NTT:
                    if step - 2 >= NTT - 2:
                        tail.append(lambda t=step - 2: ln_C(t, False, row0, b=1))
                    else:
                        ln_C(step - 2, False, row0, b=1)
                if step < NTT:
                    for c in range(8):
                        tr(PT[:, c * 128:(c + 1) * 128], YCAT[:, c * 128:(c + 1) * 128], [tYC[0], tYC[1]], [tPT[0]])
                    cp("act", YCT[:], PT[:, 0:1024].rearrange("p (c t) -> p c t", c=8), [tPT[0]], [tYCT])
                    if step + 1 < NTT:
                        front_tr(step + 1)
                    for dh in range(2):
                        for c in range(8):
                            mm(PP[dh][:, 512:1024], YCT[:, c, :], WD[0][:, c * 1024 + dh * 512: c * 1024 + (dh + 1) * 512],
                               c == 0, c == 7, [tYCT, tWD[0][c]], [tP[dh][1]])
                if 0 <= step - 1 < NTT:
                    ln_B2a(step - 1)
                if step < NTT:
                    ln_B1(step, [(PA[:, 512:1024], [tP[0][1]], 0, 512), (PB[:, 512:1024], [tP[1][1]], 512, 512)], 1.0 / ALPHA)
                if 0 <= step - 1 < NTT:
                    ln_B2b(step - 1)

        try:
          for tb in range(nblk if "castonly" not in DBG else 0):
            row0 = tb * TB
            dma("sp", TABA[:], ropeA_d[:, tb * NTT:(tb + 1) * NTT, :], sTA, (), [tTABA])
            dma("sp", TABR[:], ropeR_d[:, tb * NTT:(tb + 1) * NTT, :], sTR, (), [tTABR])
            for tti in range(NTT):
                dma("sp", X[:, tti, :], x_d[row0 + tti * 128: row0 + (tti + 1) * 128, :], sX[tti], (), [tX[tti]])
            ck("load")
            for tti in range(NTT):
                make_xt(tti)
            ck("xt")
            for l in range(NL):
                if tb == 0 and l + 1 < NL:
                    pending_casts.extend(cast_jobs(l + 1))
                ffn(l, 0, False, row0)
                ck("ffn1")
                mixer(l, tb, row0)
                ck("mixer")
                ffn(l, 1, l == NL - 1, row0)
                cast_some(len(pending_casts))
        except _Stop:
            pass
        flush_tail()
        if "castonly" in DBG:
            for key in tWS:
                p.add("sp", lambda e: e.nop(), tWS[key], [])
        p.add("sp", lambda e: e.nop(), [], tOUT)
        p.emit(nc, st)
    return nc


_CACHE = {}


def _run(inputs, S, NL, ncores):
    key = (S, NL)
    if key not in _CACHE:
        _CACHE[key] = (build_program(S, NL), _host_consts(S))
    nc, consts = _CACHE[key]
    f32 = lambda a: np.ascontiguousarray(np.asarray(a, dtype=np.float32))
    lnp = np.stack([inputs["ln1_g"], inputs["ln1_b"], inputs["ln2_g"], inputs["ln2_b"],
                    inputs["ln3_g"], inputs["ln3_b"]], axis=1)[:NL].reshape(NL * 6, D)
    shared = dict(
        f1gu=f32(inputs["ffn1_w_gu"][:NL]), f1d=f32(inputs["ffn1_w_down"][:NL]),
        f2gu=f32(inputs["ffn2_w_gu"][:NL]), f2d=f32(inputs["ffn2_w_down"][:NL]),
        win=f32(inputs["w_in"][:NL]), wout=f32(inputs["w_out"][:NL]),
        sinks=f32(np.asarray(inputs["attn_sinks"])[:NL].reshape(1, NL * 8)), lnp=f32(lnp),
    )
    shared.update(consts)
    x = np.asarray(inputs["x"], dtype=np.float32)
    in_maps = []
    for c in range(ncores):
        m = dict(shared)
        m["x"] = np.ascontiguousarray(x[c, :S])
        in_maps.append(m)
    res = run_bass_kernel_spmd(nc, in_maps, core_ids=list(range(ncores)))
    return np.stack([res.results[c]["out"] for c in range(ncores)], axis=0)


def kernel(x, w_in, w_out, attn_sinks, ffn1_w_gu, ffn1_w_down, ffn2_w_gu, ffn2_w_down,
           ln1_g, ln1_b, ln2_g, ln2_b, ln3_g, ln3_b):
    inputs = dict(x=x, w_in=w_in, w_out=w_out, attn_sinks=attn_sinks, ffn1_w_gu=ffn1_w_gu,
                  ffn1_w_down=ffn1_w_down, ffn2_w_gu=ffn2_w_gu, ffn2_w_down=ffn2_w_down,
                  ln1_g=ln1_g, ln1_b=ln1_b, ln2_g=ln2_g, ln2_b=ln2_b, ln3_g=ln3_g, ln3_b=ln3_b)
    return _run(inputs, SEQ, DEPTH, NB).astype(np.float32)
```

```python
import math
import os
from contextlib import ExitStack

import numpy as np
import concourse.bass as bass
import concourse.mybir as mybir
from concourse.bass_utils import run_bass_kernel_spmd

F32 = mybir.dt.float32
BF16 = mybir.dt.bfloat16
AF = mybir.ActivationFunctionType
ALU = mybir.AluOpType
AX = mybir.AxisListType

D = 1024
DEPTH = 4
SEQ = 4096
NB = 8
HD = 64
DFF = 2816
NJ = DFF // 128
INW = 2816
TB = 1024
NTT = TB // 128
ALPHA = (2 * DEPTH) ** 0.25
LN_EPS = 1e-5
GN_EPS = 1e-6
ROPE_THETA = 10000.0

ENGS = ("pe", "act", "dve", "pool", "sp")
DBG = os.environ.get("KDBG", "")


class _Stop(Exception):
    pass


def ck(name):
    if ("stop=" + name) in DBG:
        raise _Stop()


class T:
    __slots__ = ("name", "w", "r")

    def __init__(self, name):
        self.name = name
        self.w = None
        self.r = {}


class Ins:
    __slots__ = ("eng", "fn", "deps", "signal", "sem", "val", "is_dma", "know")

    def __init__(self, eng, fn, is_dma):
        self.eng = eng
        self.fn = fn
        self.deps = ()
        self.signal = False
        self.sem = None
        self.val = None
        self.is_dma = is_dma
        self.know = None


class DmaSem:
    __slots__ = ("name", "count", "h")

    def __init__(self, name):
        self.name = name
        self.count = 0
        self.h = None


class Prog:
    def __init__(self):
        self.q = {e: [] for e in ENGS}
        self.dma_sems = []

    def dma_sem(self, name):
        s = DmaSem(name)
        self.dma_sems.append(s)
        return s

    def add(self, eng, fn, reads=(), writes=(), dma_sem=None):
        ins = Ins(eng, fn, dma_sem is not None)
        if dma_sem is not None:
            dma_sem.count += 16
            ins.sem = dma_sem
            ins.val = dma_sem.count
        else:
            ins.sem = eng
        deps = {}
        for t in reads:
            d = t.w
            if d is not None and (d.is_dma or d.eng != eng or eng != "pe"):
                deps[id(d)] = d
        for t in writes:
            d = t.w
            if d is not None and (d.is_dma or d.eng != eng or eng != "pe"):
                deps[id(d)] = d
            for d in t.r.values():
                if d.is_dma or d.eng != eng:
                    deps[id(d)] = d
        ins.deps = tuple(deps.values())
        self.q[eng].append(ins)
        key = id(ins.sem) if ins.is_dma else eng
        for t in reads:
            t.r[key] = ins
        for t in writes:
            t.w = ins
            t.r = {}
        return ins

    def emit(self, nc, stack):
        for e in ENGS:
            for ins in self.q[e]:
                for d in ins.deps:
                    if not d.is_dma:
                        d.signal = True
        for e in ENGS:
            c = 0
            for ins in self.q[e]:
                if not ins.is_dma and ins.signal:
                    c += 1
                    ins.val = c
        esem = {e: stack.enter_context(nc.semaphore("s_" + e)) for e in ENGS}
        for s in self.dma_sems:
            s.h = stack.enter_context(nc.semaphore("d_" + s.name))

        def semh(i):
            return i.sem.h if i.is_dma else esem[i.sem]

        def semk(i):
            return id(i.sem) if i.is_dma else i.sem

        cur = {e: 0 for e in ENGS}
        known = {e: {} for e in ENGS}
        waits = {}
        progressed = True
        while progressed:
            progressed = False
            for e in ENGS:
                q = self.q[e]
                kn = known[e]
                while cur[e] < len(q):
                    ins = q[cur[e]]
                    ok = True
                    for d in ins.deps:
                        if d.know is None:
                            ok = False
                            break
                    if not ok:
                        break
                    wl = {}
                    for d in ins.deps:
                        k = semk(d)
                        if kn.get(k, 0) >= d.val:
                            continue
                        h = semh(d)
                        if wl.get(k, (None, 0))[1] < d.val:
                            wl[k] = (h, d.val)
                        for kk, vv in d.know.items():
                            if kn.get(kk, 0) < vv:
                                kn[kk] = vv
                        kn[k] = d.val
                    waits[id(ins)] = list(wl.values())
                    if ins.signal or ins.is_dma:
                        kk = dict(kn)
                        sk = semk(ins)
                        if kk.get(sk, 0) < ins.val:
                            kk[sk] = ins.val
                        ins.know = kk
                    else:
                        ins.know = kn
                    cur[e] += 1
                    progressed = True
        for e in ENGS:
            assert cur[e] == len(self.q[e]), "forward dependency on " + e

        blk = stack.enter_context(nc.Block())

        def run(eng_obj, e):
            for ins in self.q[e]:
                for h, v in waits[id(ins)]:
                    eng_obj.wait_ge(h, v)
                r = ins.fn(eng_obj)
                if ins.is_dma:
                    r.then_inc(ins.sem.h, 16)
                elif ins.signal:
                    r.then_inc(esem[e], 1)

        @blk.tensor
        def _(eng):
            run(eng, "pe")

        @blk.scalar
        def _(eng):
            run(eng, "act")

        @blk.vector
        def _(eng):
            run(eng, "dve")

        @blk.gpsimd
        def _(eng):
            run(eng, "pool")

        @blk.sync
        def _(eng):
            run(eng, "sp")


def _host_consts(S):
    nt = S // 128
    pos = np.arange(S, dtype=np.float32)
    inv_freq = (ROPE_THETA ** (-np.arange(0, HD, 2, dtype=np.float32) / HD)).astype(np.float32)
    ang = (pos[:, None] * inv_freq[None, :]).astype(np.float32)
    ropeA = np.concatenate([np.cos(ang), np.sin(ang)], axis=1).astype(np.float32)
    ang_freq = (1.0 / (ROPE_THETA ** np.linspace(0.0, 1.0, HD // 2, dtype=np.float32))).astype(np.float32)
    ang2 = (pos[:, None] * ang_freq[None, :]).astype(np.float32)
    ropeR = np.concatenate([np.cos(ang2), np.sin(ang2)], axis=1).astype(np.float32)
    ropeA = np.ascontiguousarray(ropeA.reshape(nt, 128, 64).transpose(1, 0, 2))
    ropeR = np.ascontiguousarray(ropeR.reshape(nt, 128, 64).transpose(1, 0, 2))
    H = 8
    lg = np.log1p(-np.exp2(-5.0 - np.arange(H, dtype=np.float64)))
    idx = np.arange(128, dtype=np.float64)
    scale = HD ** -0.5
    diff = idx[None, :] - idx[:, None]
    dt = np.where(diff[:, None, :] >= 0, np.exp(np.maximum(diff, 0.0)[:, None, :] * lg[None, :, None]), 0.0) * scale
    dt = dt.reshape(128, H * 128).astype(np.float32)
    wk = (np.exp((127.0 - idx)[:, None] * lg[None, :]) * scale).astype(np.float32)
    wqt = np.zeros((128, 4, 128), np.float64)
    gct = np.zeros((128, 4), np.float64)
    for c in range(4):
        for hp in range(2):
            h = 2 * c + hp
            wqt[hp * 64:(hp + 1) * 64, c, :] = np.exp((idx + 1.0) * lg[h])[None, :]
            gct[hp * 64:(hp + 1) * 64, c] = np.exp(128.0 * lg[h])
    wqt = wqt.reshape(128, 512).astype(np.float32)
    gct = gct.astype(np.float32)
    k = np.arange(128)[:, None]
    q = np.arange(128)[None, :]
    mprev = np.where(k > q, 0.0, -30000.0).astype(np.float32)
    mcur = np.where(k <= q, 0.0, -30000.0).astype(np.float32)
    masks = np.concatenate([mprev, mcur], axis=1).astype(np.float32)
    ident = np.eye(128, dtype=np.float32)
    return dict(ropeA=ropeA, ropeR=ropeR, dtab=dt, wk=wk, wqt=wqt, gct=gct, masks=masks, ident=ident)


def build_program(S=SEQ, NL=DEPTH):
    nblk = S // TB
    ntile = S // 128
    nc = bass.Bass("TRN2", target_bir_lowering=False)

    def din(name, shape, dt=F32):
        return nc.dram_tensor(name, list(shape), dt, kind="ExternalInput").ap()

    x_d = din("x", [S, D])
    f1gu_d = din("f1gu", [NL, D, 2 * DFF])
    f1d_d = din("f1d", [NL, DFF, D])
    f2gu_d = din("f2gu", [NL, D, 2 * DFF])
    f2d_d = din("f2d", [NL, DFF, D])
    win_d = din("win", [NL, D, INW])
    wout_d = din("wout", [NL, D, D])
    sinks_d = din("sinks", [1, NL * 8])
    lnp_d = din("lnp", [NL * 6, D])
    ropeA_d = din("ropeA", [128, ntile, 64])
    ropeR_d = din("ropeR", [128, ntile, 64])
    dtab_d = din("dtab", [128, 1024])
    wk_d = din("wk", [128, 8])
    wqt_d = din("wqt", [128, 512])
    gct_d = din("gct", [128, 4])
    masks_d = din("masks", [128, 256])
    ident_d = din("ident", [128, 128])
    out_d = nc.dram_tensor("out", [S, D], F32, kind="ExternalOutput").ap()

    def dscr(name, shape):
        return nc.dram_tensor(name, list(shape), BF16).ap()

    NG = NJ // 2
    wgu_s = dscr("wgu_s", [NL * 2 * NG, 128, 8 * 512])
    wd_s = dscr("wd_s", [NL * 2 * 2, 128, NJ * 512])
    win_s = dscr("win_s", [NL * 6, 128, 8 * 512])
    wout_s = dscr("wout_s", [NL, 128, 8 * 1024])

    gu_d = (f1gu_d, f2gu_d)
    dn_d = (f1d_d, f2d_d)

    IG = [(0, 512), (512, 256), (768, 512), (1280, 512), (1792, 512), (2304, 512)]

    p = Prog()
    with ExitStack() as st:
        def sb(name, shape, dt):
            return st.enter_context(nc.sbuf_tensor(name, list(shape), dt))

        X = sb("X", [128, NTT, D], F32)
        XT = sb("XT", [128, 8, TB], BF16)
        BIG = sb("BIG", [128, NJ * TB], BF16)
        NWU = 2
        WU = [sb("WU%d" % i, [128, 8, 512], BF16) for i in range(NWU)]
        WD = [sb("WD%d" % i, [128, NJ * 512], BF16) for i in range(2)]
        GB = sb("GB", [128, 2, D], F32)
        SCR = [sb("SCR%d" % i, [128, 512], F32) for i in range(2)]
        XB = SCR[1][:].bitcast(BF16)
        QAZ = [sb("QAZ%d" % i, [128, 4, 128], BF16) for i in range(2)]
        KD = [sb("KD%d" % i, [128, 2, 128], BF16) for i in range(2)]
        QRZ = [sb("QRZ%d" % i, [128, 4, 128], BF16) for i in range(2)]
        KRT = sb("KRT", [128, 4, 128], BF16)
        PTL = [[sb("PTL%d%d" % (a, b), [128, 512], BF16) for b in range(2)] for a in range(2)]
        SRB = sb("SRB", [128, 8, 128], BF16)
        QW = [sb("QW%d" % i, [128, 4, 128], BF16) for i in range(2)]
        VW = sb("VW", [128, 8, 64], BF16)
        YCAT = sb("YCAT", [128, D], BF16)
        YCT = sb("YCT", [128, 8, 128], BF16)
        TABA = sb("TABA", [128, NTT, 64], F32)
        TABR = sb("TABR", [128, NTT, 64], F32)
        DTAB = sb("DTAB", [128, 1024], F32)
        WQT = sb("WQT", [128, 512], F32)
        GCT = sb("GCT", [128, 4], F32)
        WKB = sb("WKB", [128, 8], F32)
        MASK = sb("MASK", [128, 2, 128], BF16)
        IDENT = sb("IDENT", [128, 128], BF16)
        ES = sb("ES", [128, NL * 8], F32)
        STt = [sb("ST%d" % l, [128, 256], F32) for l in range(NL)]
        SBt = [sb("SB%d" % l, [128, 4, 64], BF16) for l in range(NL)]
        CKT = [sb("CKT%d" % l, [128, 2, 128], BF16) for l in range(NL)]
        CV = [sb("CV%d" % l, [128, 2, 65], BF16) for l in range(NL)]
        VA1 = sb("VA1", [128, NTT, 2, 65], BF16)
        STATS = sb("STATS", [128, 2, 6], F32)
        MV = sb("MV", [128, 8], F32)
        GN = sb("GN", [128, 6, 8], F32)
        DEN = sb("DEN", [128, 2, 8], F32)
        EPS = sb("EPS", [128, 2], F32)
        MARK = sb("MARK", [128, 2], F32)

        PA = st.enter_context(nc.psum_tensor("PA", [128, 1024], F32))
        PB = st.enter_context(nc.psum_tensor("PB", [128, 1024], F32))
        PC = st.enter_context(nc.psum_tensor("PC", [128, 1024], F32))
        PT = st.enter_context(nc.psum_tensor("PT", [128, 2048], BF16))
        PP = [PA, PB, PC]

        ACTT = BIG[:].rearrange("p (j t) -> p j t", j=NJ)
        Hh = BIG[:].rearrange("p (t c) -> p t c", t=NTT)
        H_QA, H_KD, H_QR, H_KR, H_VR, H_SG = 0, 512, 768, 1280, 1792, 2304

        tX = [T("X%d" % i) for i in range(NTT)]
        tXT = [T("XT%d" % i) for i in range(NTT)]
        tBIG = T("BIG")
        tACT = [[T("ACT%d_%d" % (j, h)) for h in range(2)] for j in range(NJ)]
        tH = [{k: T("H%d%s" % (i, k)) for k in ("qa", "kd", "qr", "kr", "vr", "sg")} for i in range(NTT)]
        tVA1 = [T("VA1_%d" % i) for i in range(NTT)]
        tWU = [T("WU%d" % i) for i in range(NWU)]
        tWD = [[T("WD%d_%d" % (i, k)) for k in range(NG)] for i in range(2)]
        tG = T("G")
        tBv = T("B")
        tSCR = [T("SCR%d" % i) for i in range(2)]
        tQAZ = T("QAZ")
        tKD = [T("KD%d" % i) for i in range(2)]
        tQRZ = T("QRZ")
        tKRT = T("KRT")
        tPTL = [[T("PTL%d%d" % (a, b)) for b in range(2)] for a in range(2)]
        tSRB = T("SRB")
        tQW = T("QW")
        tVW = T("VW")
        tYC = [T("YCa"), T("YCr")]
        tYCT = T("YCT")
        tTABA = T("TABA")
        tTABR = T("TABR")
        tDTAB = T("cDTAB")
        tWQT = T("cWQT")
        tGCT = T("cGCT")
        tWKB = T("cWKB")
        tESr = T("cESr")
        tMASK = T("cMASK")
        tIDENT = T("cIDENT")
        tEPS = T("cEPS")
        tES = T("ES")
        tST = [T("ST%d" % l) for l in range(NL)]
        tSB = [T("SB%d" % l) for l in range(NL)]
        tCKT = [T("CKT%d" % l) for l in range(NL)]
        tCV = [T("CV%d" % l) for l in range(NL)]
        tSTATS = T("STATS")
        tMV = T("MV")
        tGN = T("GN")
        tGN2 = T("GN2")
        tDEN = T("DEN")
        tMARK = T("MARK")
        tP = [[T("P%d_%d" % (i, h)) for h in range(2)] for i in range(3)]
        tPT = [T("PT0"), T("PT1")]
        tOUT = [T("OUT%d" % i) for i in range(NTT)]

        sWU = [p.dma_sem("wu%d" % i) for i in range(NWU)]
        sWD = [[p.dma_sem("wd%d_%d" % (i, k)) for k in range(NG)] for i in range(2)]
        sG = p.dma_sem("g")
        sB = p.dma_sem("b")
        sX = [p.dma_sem("x%d" % i) for i in range(NTT)]
        sTA = p.dma_sem("ta")
        sTR = p.dma_sem("tr")

        def mm(out, lhsT, rhs, start, stop, r, w):
            p.add("pe", lambda e: e.matmul(out, lhsT=lhsT, rhs=rhs, start=start, stop=stop), r, w)

        def tr(out, in_, r, w):
            p.add("pe", lambda e: e.transpose(out, in_, IDENT[:]), list(r) + [tIDENT], w)

        def act(out, in_, func, r, w, bias=None, scale=None):
            kw = {}
            if bias is not None:
                kw["bias"] = bias
            if scale is not None:
                kw["scale"] = scale
            p.add("act", lambda e: e.activation(out=out, in_=in_, func=func, **kw), r, w)

        def tt(eng, out, in0, in1, op, r, w):
            p.add(eng, lambda e: e.tensor_tensor(out=out, in0=in0, in1=in1, op=op), r, w)

        def ts(eng, out, in0, s1, s2, op0, op1, r, w):
            if s2 is None:
                p.add(eng, lambda e: e.tensor_scalar(out=out, in0=in0, scalar1=s1, scalar2=None, op0=op0), r, w)
            else:
                p.add(eng, lambda e: e.tensor_scalar(out=out, in0=in0, scalar1=s1, scalar2=s2, op0=op0, op1=op1), r, w)

        def stt(eng, out, in0, scalar, in1, op0, op1, r, w):
            p.add(eng, lambda e: e.scalar_tensor_tensor(out=out, in0=in0, scalar=scalar, in1=in1, op0=op0, op1=op1), r, w)

        def cp(eng, out, in_, r, w):
            if eng == "act":
                p.add("act", lambda e: e.copy(out=out, in_=in_), r, w)
            else:
                p.add(eng, lambda e: e.tensor_copy(out=out, in_=in_), r, w)

        def red(eng, out, in_, r, w):
            p.add(eng, lambda e: e.tensor_reduce(out=out, in_=in_, axis=AX.X, op=ALU.add), r, w)

        def dma(eng, out, in_, sem, r, w):
            p.add(eng, lambda e: e.dma_start(out=out, in_=in_), r, w, dma_sem=sem)

        def memset(eng, ap, val, w):
            p.add(eng, lambda e: e.memset(ap, val), (), w)

        tWS = {}
        sWS = {}
        tWSg = {}
        NTHR = 6
        thr = [T("thr%d" % i) for i in range(NTHR)]
        thr_i = [0]

        def cdma(out, in_, sem, t, kind=""):
            if ("skip" + kind) in DBG:
                return
            dma("pool", out, in_, sem, (), [t])

        def cast_jobs(l):
            jobs = []

            def ffn_jobs(f):
                key = (l, "gu", f)
                sWS[key] = p.dma_sem("cgu%d_%d" % (l, f))
                tWS[key] = []
                src = gu_d[f][l].rearrange("(k q) c -> q k c", q=128)
                for g in range(NG):
                    dst = wgu_s[(l * 2 + f) * NG + g].rearrange("q (k c) -> q k c", k=8)
                    fine = (l == 0 and f == 0)
                    gsem = p.dma_sem("cg0_%d" % g) if fine else sWS[key]
                    if fine:
                        tWSg[g] = []
                    for au in range(2):
                        t = T("ws")
                        (tWSg[g] if fine else tWS[key]).append(t)
                        jobs.append(lambda dst=dst, src=src, au=au, g=g, gsem=gsem, t=t: cdma(
                            dst[:, :, au * 256:(au + 1) * 256],
                            src[:, :, au * DFF + g * 256: au * DFF + (g + 1) * 256], gsem, t, "gu"))
                key = (l, "dn", f)
                sWS[key] = p.dma_sem("cdn%d_%d" % (l, f))
                tWS[key] = []
                src = dn_d[f][l].rearrange("(j q) c -> q j c", q=128)
                for dh in range(2):
                    dst = wd_s[(l * 2 + f) * 2 + dh].rearrange("q (j c) -> q j c", j=NJ)
                    for jh in range(2):
                        t = T("ws")
                        tWS[key].append(t)
                        jobs.append(lambda dst=dst, src=src, jh=jh, dh=dh, key=key, t=t: cdma(
                            dst[:, jh * 11:(jh + 1) * 11, :],
                            src[:, jh * 11:(jh + 1) * 11, dh * 512:(dh + 1) * 512], sWS[key], t, "dn"))

            ffn_jobs(0)
            key = (l, "in", 0)
            sWS[key] = p.dma_sem("cin%d" % l)
            tWS[key] = []
            src = win_d[l].rearrange("(k q) c -> q k c", q=128)
            for gi, (c0, wd_) in enumerate(IG):
                dst = win_s[l * 6 + gi][:, 0:8 * wd_].rearrange("q (k c) -> q k c", k=8)
                t = T("ws")
                tWS[key].append(t)
                jobs.append(lambda dst=dst, src=src, c0=c0, wd_=wd_, key=key, t=t: cdma(
                    dst, src[:, :, c0:c0 + wd_], sWS[key], t, "in"))
            key = (l, "out", 0)
            sWS[key] = p.dma_sem("cout%d" % l)
            tWS[key] = []
            src = wout_d[l].rearrange("(k q) c -> q k c", q=128)
            dst = wout_s[l].rearrange("q (k c) -> q k c", k=8)
            for hh in range(2):
                t = T("ws")
                tWS[key].append(t)
                jobs.append(lambda dst=dst, src=src, hh=hh, key=key, t=t: cdma(
                    dst[:, hh * 4:(hh + 1) * 4, :], src[:, hh * 4:(hh + 1) * 4, :], sWS[key], t, "out"))
            ffn_jobs(1)
            return jobs

        pending_casts = []

        def cast_layer(l):
            for j in cast_jobs(l):
                j()

        def cast_some(n):
            for _ in range(n):
                if pending_casts:
                    pending_casts.pop(0)()

        sCs = [p.dma_sem("c%d" % i) for i in range(7)]
        dma("sp", DTAB[:], dtab_d, sCs[0], (), [tDTAB])
        dma("sp", WQT[:], wqt_d, sCs[1], (), [tWQT])
        dma("sp", GCT[:], gct_d, sCs[2], (), [tGCT])
        dma("sp", WKB[:], wk_d, sCs[3], (), [tWKB])
        dma("sp", ES[:], sinks_d.partition_broadcast(128), sCs[4], (), [tESr])
        dma("pool", MASK[:].rearrange("p b t -> p (b t)"), masks_d, sCs[5], (), [tMASK])
        dma("pool", IDENT[:], ident_d, sCs[6], (), [tIDENT])
        cast_layer(0)
        act(ES[:], ES[:], AF.Exp, [tESr], [tES, tESr])
        memset("dve", EPS[:, 0:1], LN_EPS / (ALPHA * ALPHA), [tEPS])
        memset("dve", EPS[:, 1:2], GN_EPS, [tEPS])
        memset("dve", VA1[:, :, :, 64:65], 1.0, tVA1)
        for l in range(NL):
            memset("dve", STt[l][:], 0.0, [tST[l]])
            memset("dve", SBt[l][:], 0.0, [tSB[l]])
        memset("dve", MARK[:], 0.0, [tMARK])
        for i in range(2):
            memset("dve", QAZ[i][:], 0.0, [tQAZ])
            memset("dve", QRZ[i][:], 0.0, [tQRZ])

        pp_i = [0]
        tail = []

        def flush_tail():
            while tail:
                tail.pop(0)()

        wu_i = [0]

        def next_pair():
            i = pp_i[0] % 3
            pp_i[0] += 1
            return i

        def make_xt(tti, b=None):
            if b is None:
                b = tti % 2
            cp("act", XB, X[:, tti, :], [tX[tti]], [tSCR[1]])
            for k in range(8):
                tr(PT[:, b * 1024 + k * 128: b * 1024 + (k + 1) * 128], XB[:, k * 128:(k + 1) * 128], [tSCR[1]], [tPT[b]])
            cp("act", XT[:, :, tti * 128:(tti + 1) * 128],
               PT[:, b * 1024:(b + 1) * 1024].rearrange("p (k t) -> p k t", k=8), [tPT[b]], [tXT[tti]])

        def ln_B(tti, srcs, cscale):
            ln_B1(tti, srcs, cscale)
            ln_B2(tti)

        def ln_B1(tti, srcs, cscale):
            for ap, tl, c0, n in srcs:
                stt("dve", X[:, tti, c0:c0 + n], ap, cscale, X[:, tti, c0:c0 + n], ALU.mult, ALU.add,
                    list(tl) + [tX[tti]], [tX[tti]])

        def ln_B2(tti):
            ln_B2a(tti)
            ln_B2b(tti)

        def ln_B2a(tti):
            xs = X[:, tti, :]
            p.add("dve", lambda e: e.bn_stats(out=STATS[:, 0, :], in_=X[:, tti, 0:512]), [tX[tti]], [tSTATS])
            p.add("dve", lambda e: e.bn_stats(out=STATS[:, 1, :], in_=X[:, tti, 512:1024]), [tX[tti]], [tSTATS])
            p.add("dve", lambda e: e.bn_aggr(out=MV[:, 0:2], in_=STATS[:]), [tSTATS], [tMV])
            act(MV[:, 2:3], MV[:, 1:2], AF.Ln, [tMV, tEPS], [tMV], bias=EPS[:, 0:1], scale=1.0)
            act(MV[:, 2:3], MV[:, 2:3], AF.Exp, [tMV], [tMV], scale=-0.5)
            stt("dve", MV[:, 3:4], MV[:, 0:1], -1.0, MV[:, 2:3], ALU.mult, ALU.mult, [tMV], [tMV])
            act(xs, xs, AF.Identity, [tX[tti], tMV], [tX[tti]], bias=MV[:, 3:4], scale=MV[:, 2:3])

        def ln_B2b(tti):
            xs = X[:, tti, :]
            tt("pool", xs, xs, GB[:, 0, :], ALU.mult, [tX[tti], tG], [tX[tti]])
            tt("pool", xs, xs, GB[:, 1, :], ALU.add, [tX[tti], tBv], [tX[tti]])

        def ln_C(tti, last, row0, b=None):
            if last:
                dma("sp", out_d[row0 + tti * 128: row0 + (tti + 1) * 128, :], X[:, tti, :], sX[tti], [tX[tti]], [tOUT[tti]])
            else:
                make_xt(tti, b)

        def load_ln(l, which):
            dma("sp", GB[:, 0, :], lnp_d[l * 6 + 2 * which: l * 6 + 2 * which + 1, :].partition_broadcast(128),
                sG, (), [tG])
            dma("sp", GB[:, 1, :], lnp_d[l * 6 + 2 * which + 1: l * 6 + 2 * which + 2, :].partition_broadcast(128),
                sB, (), [tBv])

        def ffn(l, f, last, row0):
            p.add("dve", lambda e: e.memset(MARK[:, 0:1], 0.0), [], [tBIG, tMARK])
            kgu = (l, "gu", f)
            kdn = (l, "dn", f)
            for g in range(NG):
                cast_some(2)
                s = wu_i[0] % NWU
                wu_i[0] += 1
                dma("sp", WU[s][:].rearrange("p k c -> p (k c)"), wgu_s[(l * 2 + f) * NG + g], sWU[s],
                    tWSg[g] if (l == 0 and f == 0) else tWS[kgu], [tWU[s]])
                if g == 2:
                    load_ln(l, 0 if f == 0 else 2)
                for dh in range(2):
                    dma("sp", WD[dh][:, g * 1024:(g + 1) * 1024], wd_s[(l * 2 + f) * 2 + dh][:, g * 1024:(g + 1) * 1024],
                        sWD[dh][g], tWS[kdn], [tWD[dh][g]])
                for half in range(2):
                    if half == 1:
                        flush_tail()
                    xr = [tXT[half * 4 + i] for i in range(4)]
                    for jj in range(2):
                        j = 2 * g + jj
                        pi = next_pair()
                        for au in range(2):
                            for k in range(8):
                                mm(PP[pi][:, au * 512:(au + 1) * 512],
                                   WU[s][:, k, au * 256 + jj * 128: au * 256 + (jj + 1) * 128],
                                   XT[:, k, half * 512:(half + 1) * 512], k == 0, k == 7,
                                   [tWU[s]] + xr, [tP[pi][au]])
                        sc = (2 * g + jj + half) % 2
                        act(SCR[sc][:], PP[pi][:, 0:512], AF.Silu, [tP[pi][0]], [tSCR[sc]])
                        tt("dve", ACTT[:, j, half * 512:(half + 1) * 512], SCR[sc][:], PP[pi][:, 512:1024], ALU.mult,
                           [tSCR[sc], tP[pi][1], tBIG], [tACT[j][half]])
            ck("up")
            pis = {}
            for step in range(NTT + 2):
                if step < NTT:
                    tti = step
                    pi = next_pair()
                    pis[tti] = pi
                    half = tti // 4
                    for j in range(NJ):
                        for dh in range(2):
                            mm(PP[pi][:, dh * 512:(dh + 1) * 512], ACTT[:, j, tti * 128:(tti + 1) * 128],
                               WD[dh][:, j * 512:(j + 1) * 512], j == 0, j == NJ - 1,
                               [tACT[j][half], tWD[dh][j // 2], tBIG], [tP[pi][dh]])
                if 0 <= step - 1 < NTT:
                    t1 = step - 1
                    ln_B(t1, [(PP[pis[t1]][:], [tP[pis[t1]][0], tP[pis[t1]][1]], 0, 1024)], 0.5 / ALPHA)
                if 0 <= step - 2 < NTT:
                    if step - 2 >= NTT - 2 and not last:
                        tail.append(lambda t=step - 2: ln_C(t, False, row0))
                    else:
                        ln_C(step - 2, last, row0)

        def f32v(ap):
            return ap.bitcast(F32)

        ROPE_SETS = [
            [(f32v(PTL[0][0][:]), tPTL[0][0]), (f32v(PTL[0][1][:]), tPTL[0][1]),
             (f32v(PTL[1][0][:]), tPTL[1][0]), (f32v(PTL[1][1][:]), tPTL[1][1])],
            [(f32v(YCAT[:, 0:512]), tYC[0]), (f32v(YCAT[:, 512:1024]), tYC[1]),
             (f32v(VW[:].rearrange("p h e -> p (h e)")), tVW), (f32v(SRB[:, 0:4, :].rearrange("p h t -> p (h t)")), tSRB)],
        ]
        rope_i = [0]

        def rope_tmps(nh):
            st_ = ROPE_SETS[rope_i[0] % 2]
            rope_i[0] += 1
            n = nh * 32
            return [(ap[:, 0:n].rearrange("p (h f) -> p h f", h=nh), tl) for ap, tl in st_]

        def rope_half(psrc, nh, dst_list, tab, tti, rd, wr):
            P4 = psrc.rearrange("p (h t f) -> p h t f", h=nh, t=2)
            x1 = P4[:, :, 0, :]
            x2 = P4[:, :, 1, :]
            Cb = tab[:, tti, 0:32].unsqueeze(1).to_broadcast([128, nh, 32])
            Sb = tab[:, tti, 32:64].unsqueeze(1).to_broadcast([128, nh, 32])
            (A, tA), (B, tB), (A2, tA2), (B2, tB2) = rope_tmps(nh)
            tt("dve", A, x1, Cb, ALU.mult, rd + [tTABA], [tA])
            tt("dve", B, x2, Sb, ALU.mult, rd + [tTABA], [tB])
            tt("dve", A2, x2, Cb, ALU.mult, rd + [tTABA], [tA2])
            tt("dve", B2, x1, Sb, ALU.mult, rd + [tTABA], [tB2])
            for dst in dst_list:
                tt("pool", dst[:, :, 0, :], A, B, ALU.subtract, [tA, tB, tBIG], wr)
                tt("pool", dst[:, :, 1, :], A2, B2, ALU.add, [tA2, tB2, tBIG], wr)

        def rope_pair(psrc, dst, tab, tti, rd, wr):
            P4 = psrc.rearrange("p (h f t) -> p h f t", h=8, t=2)
            D4 = dst.rearrange("p (h f t) -> p h f t", h=8, t=2)
            xe = P4[:, :, :, 0]
            xo = P4[:, :, :, 1]
            Cb = tab[:, tti, 0:32].unsqueeze(1).to_broadcast([128, 8, 32])
            Sb = tab[:, tti, 32:64].unsqueeze(1).to_broadcast([128, 8, 32])
            (A, tA), (B, tB), (A2, tA2), (B2, tB2) = rope_tmps(8)
            tt("dve", A, xe, Cb, ALU.mult, rd + [tTABR], [tA])
            tt("dve", B, xo, Sb, ALU.mult, rd + [tTABR], [tB])
            tt("dve", A2, xo, Cb, ALU.mult, rd + [tTABR], [tA2])
            tt("dve", B2, xe, Sb, ALU.mult, rd + [tTABR], [tB2])
            tt("pool", D4[:, :, :, 0], A, B, ALU.subtract, [tA, tB, tBIG], wr)
            tt("pool", D4[:, :, :, 1], A2, B2, ALU.add, [tA2, tB2, tBIG], wr)

        def mixer(l, tb, row0):
            p.add("dve", lambda e: e.memset(MARK[:, 1:2], 0.0), [], [tBIG, tMARK])
            kin = (l, "in", 0)
            first_blk = (tb == 0)
            hb_i = 0
            for gi, (c0, wd_) in enumerate(IG):
                cast_some(2)
                s = wu_i[0] % NWU
                wu_i[0] += 1
                if wd_ == 512:
                    dma("sp", WU[s][:].rearrange("p k c -> p (k c)"), win_s[l * 6 + gi], sWU[s], tWS[kin], [tWU[s]])
                else:
                    dma("sp", WU[s][:, :, 0:wd_], win_s[l * 6 + gi][:, 0:8 * wd_].rearrange("q (k c) -> q k c", k=8),
                        sWU[s], tWS[kin], [tWU[s]])
                if gi == 2:
                    load_ln(l, 1)
                if 1 <= gi <= 4:
                    q4 = gi - 1
                    dma("sp", WD[0][:, q4 * 2048:(q4 + 1) * 2048], wout_s[l][:, q4 * 2048:(q4 + 1) * 2048],
                        sWD[0][2 * q4], tWS[(l, "out", 0)], [tWD[0][2 * q4], tWD[0][2 * q4 + 1]])
                for tti in range(NTT):
                    if tti == NTT - 2:
                        flush_tail()
                    pi, hb = (hb_i // 2) % 3, hb_i % 2
                    hb_i += 1
                    ps = PP[pi][:, hb * 512: hb * 512 + wd_]
                    tp = tP[pi][hb]
                    for k in range(8):
                        mm(ps, XT[:, k, tti * 128:(tti + 1) * 128], WU[s][:, k, 0:wd_], k == 0, k == 7,
                           [tWU[s], tXT[tti]], [tp])
                    hrow = Hh[:, tti, :]
                    if gi == 0:
                        rope_half(ps, 8, [hrow[:, H_QA:H_QA + 512].rearrange("p (h t f) -> p h t f", h=8, t=2)],
                                  TABA, tti, [tp], [tH[tti]["qa"]])
                    elif gi == 1:
                        kd = hrow[:, H_KD:H_KD + 256].rearrange("p (v r t f) -> p v r t f", v=2, r=2, t=2)
                        rope_half(ps[:, 0:128], 2, [kd[:, :, 0, :, :], kd[:, :, 1, :, :]], TABA, tti, [tp], [tH[tti]["kd"]])
                        cp("act", VA1[:, tti, :, 0:64], ps[:, 128:256].rearrange("p (v e) -> p v e", v=2), [tp], [tVA1[tti]])
                    elif gi == 2:
                        rope_pair(ps, hrow[:, H_QR:H_QR + 512], TABR, tti, [tp], [tH[tti]["qr"]])
                    elif gi == 3:
                        rope_pair(ps, hrow[:, H_KR:H_KR + 512], TABR, tti, [tp], [tH[tti]["kr"]])
                    elif gi == 4:
                        cp("act", hrow[:, H_VR:H_VR + 512], ps, [tp, tBIG], [tH[tti]["vr"]])
                    else:
                        act(hrow[:, H_SG:H_SG + 512], ps, AF.Silu, [tp, tBIG], [tH[tti]["sg"]])
            ck("inproj")
            def kv_prev(tti):
                qs = tti % 2
                if tti > 0:
                    return (lambda v: KD[1 - qs][:, v, :]), tKD[1 - qs], (lambda v: VA1[:, tti - 1, v, :]), tVA1[tti - 1]
                return (lambda v: CKT[l][:, v, :]), tCKT[l], (lambda v: CV[l][:, v, :]), tCV[l]

            def front_tr(tti):
                hrow = Hh[:, tti, :]
                th = tH[tti]
                qs = tti % 2
                for c in range(8):
                    col = H_QR + c * 128 if c < 4 else H_KR + (c - 4) * 128
                    tr(PT[:, 1024 + c * 128: 1024 + (c + 1) * 128], hrow[:, col:col + 128],
                       [th["qr"] if c < 4 else th["kr"], tBIG], [tPT[1]])
                ptr = PT[:, 1024:2048].rearrange("p (c t) -> p c t", c=8)
                cp("dve", QRZ[0][0:64, :, :], ptr[0:64, 0:4, :], [tPT[1]], [tQRZ])
                cp("dve", QRZ[1][64:128, :, :], ptr[64:128, 0:4, :], [tPT[1]], [tQRZ])
                cp("dve", KRT[:], ptr[:, 4:8, :], [tPT[1]], [tKRT])
                for par in range(2):
                    tt("pool", QW[par][:].rearrange("p c t -> p (c t)"), QRZ[par][:].rearrange("p c t -> p (c t)"), WQT[:], ALU.mult,
                       [tQRZ, tWQT], [tQW])
                for c in range(6):
                    col = H_QA + c * 128 if c < 4 else H_KD + (c - 4) * 128
                    tr(PT[:, c * 128:(c + 1) * 128], hrow[:, col:col + 128], [th["qa"] if c < 4 else th["kd"], tBIG], [tPT[0]])
                pta = PT[:, 0:768].rearrange("p (c t) -> p c t", c=6)
                cp("act", QAZ[0][0:64, :, :], pta[0:64, 0:4, :], [tPT[0]], [tQAZ])
                cp("act", QAZ[1][64:128, :, :], pta[64:128, 0:4, :], [tPT[0]], [tQAZ])
                cp("act", KD[qs][:], pta[:, 4:6, :], [tPT[0]], [tKD[qs]])

            def front(tti):
                gt = tb * NTT + tti
                hrow = Hh[:, tti, :]
                th = tH[tti]
                qs = tti % 2
                has_prev = gt > 0
                kprev, tkprev, vprev, tvprev = kv_prev(tti)
                blks = ([0] if has_prev else []) + [1]
                for h in range(8):
                    c, par = h // 2, h % 2
                    mm(PC[:, h * 128:(h + 1) * 128], KRT[:, c, :], QRZ[par][:, c, :],
                       True, True, [tKRT, tQRZ], [tP[2][h // 4]])
                tt("dve", SRB[:].rearrange("p h t -> p (h t)"), PC[:], DTAB[:], ALU.mult, [tP[2][0], tP[2][1], tDTAB], [tSRB])
                tt("pool", VW[:], hrow[:, H_VR:H_VR + 512].rearrange("p (h e) -> p h e", h=8),
                   WKB[:].unsqueeze(2).to_broadcast([128, 8, 64]), ALU.mult, [th["vr"], tBIG, tWKB], [tVW])
                for kvh in range(2):
                    for blk in blks:
                        for slot in range(4):
                            cc, par = slot // 2, slot % 2
                            chunk = kvh * 2 + cc
                            kT = kprev(kvh) if blk == 0 else KD[qs][:, kvh, :]
                            o_ap = PP[kvh][:, blk * 512 + slot * 128: blk * 512 + (slot + 1) * 128]
                            mm(o_ap, kT, QAZ[par][:, chunk, :], slot == 0, False,
                               [tkprev if blk == 0 else tKD[qs], tQAZ], [tP[kvh][blk]])
                            mm(o_ap, IDENT[:], MASK[:, blk, :], False, slot == 3, [tIDENT, tMASK], [tP[kvh][blk]])
                for kvh in range(2):
                    for blk in blks:
                        act(PTL[kvh][blk][:], PP[kvh][:, blk * 512:(blk + 1) * 512], AF.Exp, [tP[kvh][blk]], [tPTL[kvh][blk]],
                            scale=HD ** -0.5)
                for h in range(8):
                    c, par = h // 2, h % 2
                    mm(PC[:, h * 64:(h + 1) * 64], SRB[:, h, :], hrow[:, H_VR + h * 64: H_VR + (h + 1) * 64], True, False,
                       [tSRB, th["vr"], tBIG], [tP[2][0]])
                    mm(PC[:, h * 64:(h + 1) * 64], QW[par][:, c, :], SBt[l][:, c, :], False, True,
                       [tQW, tSB[l]], [tP[2][0]])
                for c in range(4):
                    mm(PC[:, 512 + c * 128: 512 + (c + 1) * 128], hrow[:, H_KR + c * 128: H_KR + (c + 1) * 128],
                       VW[:, 2 * c:2 * c + 2, :].rearrange("p h e -> p (h e)"), True, True, [th["kr"], tBIG, tVW], [tP[2][1]])
                for kvh in range(2):
                    for slot in range(4):
                        for bi, blk in enumerate(blks):
                            v1 = vprev(kvh) if blk == 0 else VA1[:, tti, kvh, :]
                            mm(PP[kvh][:, slot * 65:(slot + 1) * 65],
                               PTL[kvh][blk][:, slot * 128:(slot + 1) * 128], v1, bi == 0, bi == len(blks) - 1,
                               [tPTL[kvh][blk], tvprev if blk == 0 else tVA1[tti]], [tP[kvh][0]])
                pr3 = PC[:, 0:512].rearrange("p (h e) -> p h e", h=8)
                red("dve", GN[:, 0, :], pr3, [tP[2][0]], [tGN])
                act(SCR[0][:], PC[:, 0:512], AF.Square, [tP[2][0], tGN], [tSCR[0]])
                red("dve", GN[:, 1, :], SCR[0][:].rearrange("p (h e) -> p h e", h=8), [tSCR[0]], [tGN])
                ts("dve", GN[:, 2, :], GN[:, 0, :], 1.0 / 64, None, ALU.mult, None, [tGN], [tGN])
                tt("dve", GN[:, 5, :], GN[:, 2, :], GN[:, 2, :], ALU.mult, [tGN], [tGN])
                stt("dve", GN[:, 3, :], GN[:, 1, :], 1.0 / 64, GN[:, 5, :], ALU.mult, ALU.subtract, [tGN], [tGN])
                act(GN[:, 3, :], GN[:, 3, :], AF.Ln, [tGN, tEPS], [tGN], bias=EPS[:, 1:2], scale=1.0)
                act(GN[:, 4, :], GN[:, 3, :], AF.Exp, [tGN], [tGN2], scale=-0.5)
                t1 = SCR[0][:].rearrange("p (h e) -> p h e", h=8)
                tt("dve", t1, pr3, GN[:, 2, :].unsqueeze(2).to_broadcast([128, 8, 64]), ALU.subtract, [tP[2][0], tGN], [tSCR[0]])
                tt("dve", SCR[0][:], SCR[0][:], hrow[:, H_SG:H_SG + 512], ALU.mult, [tSCR[0], th["sg"], tBIG], [tSCR[0]])
                for kvh in range(2):
                    po = PP[kvh][:, 0:260].rearrange("p (s e) -> p s e", s=4)
                    tt("dve", DEN[:, 0, kvh * 4:(kvh + 1) * 4], po[:, :, 64], ES[:, l * 8 + kvh * 4: l * 8 + kvh * 4 + 4], ALU.add,
                       [tP[kvh][0], tES], [tDEN])
                    p.add("dve", lambda e, kvh=kvh: e.reciprocal(out=DEN[:, 1, kvh * 4:(kvh + 1) * 4], in_=DEN[:, 0, kvh * 4:(kvh + 1) * 4]),
                          [tDEN], [tDEN])
                tt("dve", YCAT[:, 512:1024].rearrange("p (h e) -> p h e", h=8), t1,
                   GN[:, 4, :].unsqueeze(2).to_broadcast([128, 8, 64]), ALU.mult, [tSCR[0], tGN2], [tYC[1]])
                for kvh in range(2):
                    po = PP[kvh][:, 0:260].rearrange("p (s e) -> p s e", s=4)
                    tt("dve", YCAT[:, kvh * 256:(kvh + 1) * 256].rearrange("p (s e) -> p s e", s=4), po[:, :, 0:64],
                       DEN[:, 1, kvh * 4:(kvh + 1) * 4].unsqueeze(2).to_broadcast([128, 4, 64]), ALU.mult,
                       [tP[kvh][0], tDEN], [tYC[0]])
                if tti == NTT - 1 and tb < nblk - 1:
                    cp("pool", CKT[l][:], KD[qs][:], [tKD[qs]], [tCKT[l]])
                    cp("pool", CV[l][:], VA1[:, tti, :, :], [tVA1[tti]], [tCV[l]])
                pkv = PC[:, 512:1024].rearrange("p (c m) -> p c m", c=4)
                st3 = STt[l][:].rearrange("p (c e) -> p c e", c=4)
                tt("pool", st3, st3, GCT[:].unsqueeze(2).to_broadcast([128, 4, 64]), ALU.mult, [tST[l], tGCT], [tST[l]])
                tt("dve", st3[0:64], st3[0:64], pkv[0:64, :, 0:64], ALU.add, [tST[l], tP[2][1]], [tST[l]])
                tt("dve", st3[64:128], st3[64:128], pkv[64:128, :, 64:128], ALU.add, [tST[l], tP[2][1]], [tST[l]])
                cp("pool", SBt[l][:], st3, [tST[l]], [tSB[l]])

            def out_stage(tti):
                for c in range(8):
                    tr(PT[:, c * 128:(c + 1) * 128], YCAT[:, c * 128:(c + 1) * 128], [tYC[0], tYC[1]], [tPT[0]])
                cp("act", YCT[:], PT[:, 0:1024].rearrange("p (c t) -> p c t", c=8), [tPT[0]], [tYCT])
                for dh in range(2):
                    for c in range(8):
                        mm(PP[dh][:, 512:1024], YCT[:, c, :], WD[0][:, c * 1024 + dh * 512: c * 1024 + (dh + 1) * 512],
                           c == 0, c == 7, [tYCT, tWD[0][c]], [tP[dh][1]])

            front_tr(0)
            for step in range(NTT + 2):
                if step < NTT:
                    front(step)
                    ck("f_%d" % step)
                if 0 <= step - 2 < NTT:
                    if step - 2 >= NTT - 2:
                        tail.append(lambda t=step - 2: ln_C(t, False, row0, b=1))
                    else:
                        ln_C(step - 2, False, row0, b=1)
                if step < NTT:
                    for c in range(8):
                        tr(PT[:, c * 128:(c + 1) * 128], YCAT[:, c * 128:(c + 1) * 128], [tYC[0], tYC[1]], [tPT[0]])
                    cp("act", YCT[:], PT[:, 0:1024].rearrange("p (c t) -> p c t", c=8), [tPT[0]], [tYCT])
                    if step + 1 < NTT:
                        front_tr(step + 1)
                    for dh in range(2):
                        for c in range(8):
                            mm(PP[dh][:, 512:1024], YCT[:, c, :], WD[0][:, c * 1024 + dh * 512: c * 1024 + (dh + 1) * 512],
                               c == 0, c == 7, [tYCT, tWD[0][c]], [tP[dh][1]])
                if 0 <= step - 1 < NTT:
                    ln_B2a(step - 1)
                if step < NTT:
                    ln_B1(step, [(PA[:, 512:1024], [tP[0][1]], 0, 512), (PB[:, 512:1024], [tP[1][1]], 512, 512)], 1.0 / ALPHA)
                if 0 <= step - 1 < NTT:
                    ln_B2b(step - 1)

        try:
          for tb in range(nblk if "castonly" not in DBG else 0):
            row0 = tb * TB
            dma("sp", TABA[:], ropeA_d[:, tb * NTT:(tb + 1) * NTT, :], sTA, (), [tTABA])
            dma("sp", TABR[:], ropeR_d[:, tb * NTT:(tb + 1) * NTT, :], sTR, (), [tTABR])
            for tti in range(NTT):
                dma("sp", X[:, tti, :], x_d[row0 + tti * 128: row0 + (tti + 1) * 128, :], sX[tti], (), [tX[tti]])
            ck("load")
            for tti in range(NTT):
                make_xt(tti)
            ck("xt")
            for l in range(NL):
                if tb == 0 and l + 1 < NL:
                    pending_casts.extend(cast_jobs(l + 1))
                ffn(l, 0, False, row0)
                ck("ffn1")
                mixer(l, tb, row0)
                ck("mixer")
                ffn(l, 1, l == NL - 1, row0)
                cast_some(len(pending_casts))
        except _Stop:
            pass
        flush_tail()
        if "castonly" in DBG:
            for key in tWS:
                p.add("sp", lambda e: e.nop(), tWS[key], [])
        p.add("sp", lambda e: e.nop(), [], tOUT)
        p.emit(nc, st)
    return nc


_CACHE = {}


def _run(inputs, S, NL, ncores):
    key = (S, NL)
    if key not in _CACHE:
        _CACHE[key] = (build_program(S, NL), _host_consts(S))
    nc, consts = _CACHE[key]
    f32 = lambda a: np.ascontiguousarray(np.asarray(a, dtype=np.float32))
    lnp = np.stack([inputs["ln1_g"], inputs["ln1_b"], inputs["ln2_g"], inputs["ln2_b"],
                    inputs["ln3_g"], inputs["ln3_b"]], axis=1)[:NL].reshape(NL * 6, D)
    shared = dict(
        f1gu=f32(inputs["ffn1_w_gu"][:NL]), f1d=f32(inputs["ffn1_w_down"][:NL]),
        f2gu=f32(inputs["ffn2_w_gu"][:NL]), f2d=f32(inputs["ffn2_w_down"][:NL]),
        win=f32(inputs["w_in"][:NL]), wout=f32(inputs["w_out"][:NL]),
        sinks=f32(np.asarray(inputs["attn_sinks"])[:NL].reshape(1, NL * 8)), lnp=f32(lnp),
    )
    shared.update(consts)
    x = np.asarray(inputs["x"], dtype=np.float32)
    in_maps = []
    for c in range(ncores):
        m = dict(shared)
        m["x"] = np.ascontiguousarray(x[c, :S])
        in_maps.append(m)
    res = run_bass_kernel_spmd(nc, in_maps, core_ids=list(range(ncores)))
    return np.stack([res.results[c]["out"] for c in range(ncores)], axis=0)


def kernel(x, w_in, w_out, attn_sinks, ffn1_w_gu, ffn1_w_down, ffn2_w_gu, ffn2_w_down,
           ln1_g, ln1_b, ln2_g, ln2_b, ln3_g, ln3_b):
    inputs = dict(x=x, w_in=w_in, w_out=w_out, attn_sinks=attn_sinks, ffn1_w_gu=ffn1_w_gu,
                  ffn1_w_down=ffn1_w_down, ffn2_w_gu=ffn2_w_gu, ffn2_w_down=ffn2_w_down,
                  ln1_g=ln1_g, ln1_b=ln1_b, ln2_g=ln2_g, ln2_b=ln2_b, ln3_g=ln3_g, ln3_b=ln3_b)
    return _run(inputs, SEQ, DEPTH, NB).astype(np.float32)
```
